# Optimizing a Trainium2 kernel written in Bass

```python
import math
import jax, jax.numpy as jnp
from jax import lax
import numpy as np

D_MODEL = 4096
BATCH = 2
SEQ = 4096
DEPTH = 4

D_FF = 3 * D_MODEL // 4
ATTN_HEADS = 8
ATTN_HEAD_DIM = 128
ATTN_V_DIM = 2 * ATTN_HEAD_DIM
ATTN_Q_BLOCK = 128
ROPE_THETA = 10000.0
GLA_HEADS = 4
GLA_K_DIM = 128
GLA_V_DIM = 256
GLA_GATE_RANK = 16
GLA_GATE_NORMALIZER = 16.0
HGRN_HEADS = 8
HGRN_K_DIM = 128
HGRN_V_DIM = 128
CHUNK = 64
NORM_EPS = 1e-6
LB_FLOOR = 1e-20

ATTN_QK_WIDTH = ATTN_HEADS * 2 * ATTN_HEAD_DIM
ATTN_WIDTH = ATTN_HEADS * ATTN_V_DIM
GLA_K_WIDTH = GLA_HEADS * GLA_K_DIM
GLA_WIDTH = GLA_HEADS * GLA_V_DIM
HGRN_K_WIDTH = HGRN_HEADS * HGRN_K_DIM
HGRN_WIDTH = HGRN_HEADS * HGRN_V_DIM

PROJ_SIZES = (
    ATTN_QK_WIDTH, ATTN_QK_WIDTH, ATTN_WIDTH,
    GLA_K_WIDTH, GLA_K_WIDTH, GLA_WIDTH, GLA_WIDTH,
    GLA_GATE_RANK, GLA_GATE_RANK,
    HGRN_K_WIDTH, HGRN_K_WIDTH, HGRN_K_WIDTH, HGRN_WIDTH, HGRN_WIDTH,
    D_MODEL, D_MODEL, D_MODEL,
)
PROJ_SPLITS = tuple(int(s) for s in np.cumsum(PROJ_SIZES)[:-1])
PROJ_TOTAL = int(sum(PROJ_SIZES))

kernel_name = "hybrid_diffattn_gla_hgrn2_macaron_encoder"


def rms_norm(x, gain):
    xf = x.astype(jnp.float32)
    y = xf * lax.rsqrt(jnp.mean(xf * xf, axis=-1, keepdims=True) + NORM_EPS)
    return (y * gain.astype(jnp.float32)).astype(x.dtype)


def swiglu_ffn(x, w_in, w_out):
    gate, up = jnp.split(x @ w_in, 2, axis=-1)
    return (jax.nn.silu(gate) * up) @ w_out


def rotary(x, positions):
    dh = x.shape[-1]
    half = dh // 2
    inv_freq = ROPE_THETA ** (-jnp.arange(half, dtype=jnp.float32) / half)
    ang = positions.astype(jnp.float32)[:, None] * inv_freq[None, :]
    cos = jnp.cos(ang)[:, None, None, :]
    sin = jnp.sin(ang)[:, None, None, :]
    xf = x.astype(jnp.float32)
    x1, x2 = xf[..., :half], xf[..., half:]
    return jnp.concatenate([x1 * cos - x2 * sin, x2 * cos + x1 * sin], axis=-1).astype(x.dtype)


def diff_attention(q, k, v, lam):
    b, s, h, _, dh = q.shape
    nb = s // ATTN_Q_BLOCK
    qb = q.reshape(b, nb, ATTN_Q_BLOCK, h, 2, dh).transpose(1, 0, 2, 3, 4, 5)
    kf = k.astype(jnp.float32)
    vf = v.astype(jnp.float32)
    scale = dh ** -0.5

    def one_block(q_blk):
        scores = jnp.einsum('bqhmd,bkhmd->bhmqk', q_blk.astype(jnp.float32), kf) * scale
        p = jax.nn.softmax(scores, axis=-1)
        a = p[:, :, 0] - lam * p[:, :, 1]
        return jnp.einsum('bhqk,bkhe->bqhe', a, vf)

    o = lax.map(one_block, qb)
    return o.transpose(1, 0, 2, 3, 4).reshape(b, s, h, -1).astype(v.dtype)


def chunked_gated_scan(q, k, v, log_f):
    b, h, s, dk = q.shape
    dv = v.shape[-1]
    n = s // CHUNK

    def chunks(t):
        return jnp.moveaxis(t.astype(jnp.float32).reshape(b, h, n, CHUNK, t.shape[-1]), 2, 0)

    mask = jnp.tril(jnp.ones((CHUNK, CHUNK), dtype=bool))[:, :, None]

    def step(state, inp):
        qc, kc, vc, gc = inp
        cum = jnp.cumsum(gc, axis=-2)
        diff = jnp.where(mask, cum[..., :, None, :] - cum[..., None, :, :], 0.0)
        decay = jnp.where(mask, jnp.exp(diff), 0.0)
        scores = jnp.einsum('bhik,bhjk,bhijk->bhij', qc, kc, decay)
        o = (jnp.einsum('bhij,bhjv->bhiv', scores, vc)
             + jnp.einsum('bhik,bhkv->bhiv', qc * jnp.exp(cum), state))
        last = cum[..., -1, :]
        state = (state * jnp.exp(last)[..., None]
                 + jnp.einsum('bhjk,bhjv->bhkv', kc * jnp.exp(last[..., None, :] - cum), vc))
        return state, o

    state0 = jnp.zeros((b, h, dk, dv), jnp.float32)
    _, o = lax.scan(step, state0, (chunks(q), chunks(k), chunks(v), chunks(log_f)))
    return jnp.moveaxis(o, 0, 2).reshape(b, h, s, dv).astype(v.dtype)


def bidirectional_scan(q, k_fwd, k_bwd, v, g_fwd, g_bwd):
    flip = lambda t: jnp.flip(t, axis=2)
    fwd = chunked_gated_scan(q, k_fwd, v, g_fwd)
    bwd = flip(chunked_gated_scan(flip(q), flip(k_bwd), flip(v), flip(g_bwd)))
    return fwd + bwd


def to_heads(t, n_heads):
    b, s, _ = t.shape
    return t.reshape(b, s, n_heads, -1).transpose(0, 2, 1, 3)


def layer_lower_bounds(logits):
    p = jax.nn.softmax(logits.astype(jnp.float32), axis=0)
    return jnp.cumsum(p, axis=0) - p[0]


def hgrn_gate(z, lb):
    zf = z.astype(jnp.float32)
    lb = jnp.clip(lb, 0.0, 1.0 - 1e-6)
    log_f = jnp.logaddexp(jnp.log(jnp.maximum(lb, LB_FLOOR)),
                          jnp.log1p(-lb) + jax.nn.log_sigmoid(zf))
    key_in = (1.0 - lb) * jax.nn.sigmoid(-zf)
    return key_in, log_f


def gla_log_gate(lr, w2, bias):
    return jax.nn.log_sigmoid((lr @ w2 + bias).astype(jnp.float32)) / GLA_GATE_NORMALIZER


def setup_inputs(seed: int = 0) -> dict:
    key = jax.random.key(seed)
    ks = jax.random.split(key, 32)
    L, D, F = DEPTH, D_MODEL, D_FF

    def w(k, shape, fan_in):
        return jax.random.normal(k, shape, jnp.float32) * (fan_in ** -0.5)

    def gain(k, shape):
        return 1.0 + 0.02 * jax.random.normal(k, shape, jnp.float32)

    def small(k, shape, s):
        return s * jax.random.normal(k, shape, jnp.float32)

    return {
        "x": jax.random.normal(ks[0], (BATCH, SEQ, D), jnp.float32),
        "ffn1_norm": gain(ks[1], (L, D)),
        "ffn1_w_in": w(ks[2], (L, D, 2 * F), D),
        "ffn1_w_out": w(ks[3], (L, F, D), F),
        "mix_norm": gain(ks[4], (L, D)),
        "w_in": w(ks[5], (L, D, PROJ_TOTAL), D),
        "attn_q_norm": gain(ks[6], (L, ATTN_HEAD_DIM)),
        "attn_k_norm": gain(ks[7], (L, ATTN_HEAD_DIM)),
        "attn_lambda": small(ks[8], (L, 4, ATTN_HEAD_DIM), 0.1),
        "attn_sub_norm": gain(ks[9], (L, ATTN_V_DIM)),
        "gla_gate_w2_fwd": w(ks[10], (L, GLA_GATE_RANK, GLA_K_WIDTH), GLA_GATE_RANK),
        "gla_gate_b_fwd": small(ks[11], (L, GLA_K_WIDTH), 0.01),
        "gla_gate_w2_bwd": w(ks[12], (L, GLA_GATE_RANK, GLA_K_WIDTH), GLA_GATE_RANK),
        "gla_gate_b_bwd": small(ks[13], (L, GLA_K_WIDTH), 0.01),
        "gla_out_norm": gain(ks[14], (L, GLA_V_DIM)),
        "hgrn_lb_fwd": small(ks[15], (L, HGRN_K_WIDTH), 0.5),
        "hgrn_lb_bwd": small(ks[16], (L, HGRN_K_WIDTH), 0.5),
        "hgrn_out_norm": gain(ks[17], (L, HGRN_V_DIM)),
        "w_branch_attn": w(ks[18], (L, ATTN_WIDTH, D), ATTN_WIDTH),
        "w_branch_gla": w(ks[19], (L, GLA_WIDTH, D), GLA_WIDTH),
        "w_branch_hgrn": w(ks[20], (L, HGRN_WIDTH, D), HGRN_WIDTH),
        "w_out": w(ks[21], (L, D, D), D),
        "ffn2_norm": gain(ks[22], (L, D)),
        "ffn2_w_in": w(ks[23], (L, D, 2 * F), D),
        "ffn2_w_out": w(ks[24], (L, F, D), F),
    }


def reference(x, ffn1_norm, ffn1_w_in, ffn1_w_out, mix_norm, w_in,
              attn_q_norm, attn_k_norm, attn_lambda, attn_sub_norm,
              gla_gate_w2_fwd, gla_gate_b_fwd, gla_gate_w2_bwd, gla_gate_b_bwd, gla_out_norm,
              hgrn_lb_fwd, hgrn_lb_bwd, hgrn_out_norm,
              w_branch_attn, w_branch_gla, w_branch_hgrn, w_out,
              ffn2_norm, ffn2_w_in, ffn2_w_out):
    b, s, _ = x.shape
    positions = jnp.arange(s, dtype=jnp.int32)
    lb_fwd_all = layer_lower_bounds(hgrn_lb_fwd)
    lb_bwd_all = layer_lower_bounds(hgrn_lb_bwd)

    for l in range(DEPTH):
        x = x + 0.5 * swiglu_ffn(rms_norm(x, ffn1_norm[l]), ffn1_w_in[l], ffn1_w_out[l])

        h = rms_norm(x, mix_norm[l])
        (a_q, a_k, a_v, g_q, g_k, g_v, g_r, g_lr_f, g_lr_b,
         h_q, h_zf, h_zb, h_i, h_g, gate_a, gate_g, gate_h) = jnp.split(h @ w_in[l], PROJ_SPLITS, axis=-1)

        q = a_q.reshape(b, s, ATTN_HEADS, 2, ATTN_HEAD_DIM)
        k = a_k.reshape(b, s, ATTN_HEADS, 2, ATTN_HEAD_DIM)
        q = rotary(rms_norm(q, attn_q_norm[l]), positions)
        k = rotary(rms_norm(k, attn_k_norm[l]), positions)
        v = a_v.reshape(b, s, ATTN_HEADS, ATTN_V_DIM)
        lam_vec = attn_lambda[l].astype(jnp.float32)
        lam_init = 0.8 - 0.6 * math.exp(-0.3 * l)
        lam = (jnp.exp(jnp.sum(lam_vec[0] * lam_vec[1]))
               - jnp.exp(jnp.sum(lam_vec[2] * lam_vec[3])) + lam_init)
        o_a = rms_norm(diff_attention(q, k, v, lam), attn_sub_norm[l]) * (1.0 - lam_init)
        y_a = o_a.reshape(b, s, ATTN_WIDTH) @ w_branch_attn[l]

        gq = to_heads(g_q, GLA_HEADS) * (GLA_K_DIM ** -0.5)
        gk = to_heads(g_k, GLA_HEADS)
        gv = to_heads(g_v, GLA_HEADS)
        gf = to_heads(gla_log_gate(g_lr_f, gla_gate_w2_fwd[l], gla_gate_b_fwd[l]), GLA_HEADS)
        gb = to_heads(gla_log_gate(g_lr_b, gla_gate_w2_bwd[l], gla_gate_b_bwd[l]), GLA_HEADS)
        o_g = bidirectional_scan(gq, gk, gk, gv, gf, gb).transpose(0, 2, 1, 3)
        o_g = rms_norm(o_g, gla_out_norm[l]).reshape(b, s, GLA_WIDTH) * jax.nn.silu(g_r)
        y_g = o_g @ w_branch_gla[l]

        k_f, logf_f = hgrn_gate(h_zf, lb_fwd_all[l])
        k_b, logf_b = hgrn_gate(h_zb, lb_bwd_all[l])
        hq = to_heads(h_q, HGRN_HEADS) * (HGRN_K_DIM ** -0.5)
        hi = to_heads(h_i, HGRN_HEADS)
        o_h = bidirectional_scan(hq, to_heads(k_f, HGRN_HEADS), to_heads(k_b, HGRN_HEADS), hi,
                                 to_heads(logf_f, HGRN_HEADS), to_heads(logf_b, HGRN_HEADS))
        o_h = rms_norm(o_h.transpose(0, 2, 1, 3), hgrn_out_norm[l]).reshape(b, s, HGRN_WIDTH) * jax.nn.silu(h_g)
        y_h = o_h @ w_branch_hgrn[l]

        merged = (jax.nn.sigmoid(gate_a) * y_a + jax.nn.sigmoid(gate_g) * y_g
                  + jax.nn.sigmoid(gate_h) * y_h)
        x = x + merged @ w_out[l]

        x = x + 0.5 * swiglu_ffn(rms_norm(x, ffn2_norm[l]), ffn2_w_in[l], ffn2_w_out[l])
    return x
```

```python
import contextlib
import math
import numpy as np
import concourse.bass as bass
import concourse.mybir as mybir
from concourse.bass_utils import run_bass_kernel_spmd

F32 = mybir.dt.float32
BF16 = mybir.dt.bfloat16
AF = mybir.ActivationFunctionType
ALU = mybir.AluOpType
EPS = 1e-6
KD = 8
STOP = None
ENG = ['pe', 'act', 'dve', 'pool', 'sp']


class Cfg:
    def __init__(s, D=4096, S=4096, L=4, F=3072):
        s.D, s.S, s.L, s.F = D, S, L, F
        s.T = S // 4
        s.TB = min(512, s.T)
        s.KC = D // 128
        s.FC = F // 128
        s.NTB = s.T // s.TB
        s.NSB = S // s.TB
        s.QG = min(512, S)
        s.NFM = 22
        s.CHB = (1 << 20) if D >= 2048 else (1 << 16)


class Buf:
    def __init__(self, t, name):
        self.t, self.name = t, name
        self.w = None
        self.r = {}
        self.parts = {}
        self.parent = None

    def part(self, key):
        if key not in self.parts:
            b = Buf(self.t, "%s.%s" % (self.name, key))
            b.parent = self
            self.parts[key] = b
        return self.parts[key]

    def __getitem__(self, idx):
        return self.t[idx]

    def wev(self):
        ev = [self.w] if self.w else []
        if self.parent is not None and self.parent.w:
            ev.append(self.parent.w)
        for p in self.parts.values():
            if p.w:
                ev.append(p.w)
        return ev

    def rev(self):
        ev = list(self.r.items())
        if self.parent is not None:
            ev += list(self.parent.r.items())
        for p in self.parts.values():
            ev += list(p.r.items())
        return ev


class K:
    def __init__(self, nc):
        self.nc = nc
        self.es = contextlib.ExitStack()
        self.e = {'pe': nc.tensor, 'act': nc.scalar, 'dve': nc.vector, 'pool': nc.gpsimd, 'sp': nc.sync}
        self.semobj = {}
        for k in ENG:
            self.semobj[k] = self.es.enter_context(nc.semaphore("s_" + k))
        self.cnt = {k: 0 for k in ENG}
        self.known = {k: {} for k in ENG}
        self.dq = {}
        for q in ('sp', 'pool'):
            for i in range(KD):
                self.semobj[('d', q, i)] = self.es.enter_context(nc.semaphore("d_%s%d" % (q, i)))
            self.dq[q] = 0
        self.ncc = 0
        self.NCS = 4
        for i in range(self.NCS):
            self.semobj[('cc', i)] = self.es.enter_context(nc.semaphore("cc%d" % i))
        self.nps = 0
        self.ps = []
        for i in range(8):
            t = self.es.enter_context(nc.psum_tensor("psb%d" % i, [128, 512], F32))
            self.ps.append(Buf(t, "ps%d" % i))
        self.uid = 0

    def psum(self):
        b = self.ps[self.nps % 8]
        self.nps += 1
        return b

    def _wait(self, eng, key, val):
        if self.known[eng].get(key, 0) >= val:
            return
        self.e[eng].wait_ge(self.semobj[key], val)
        self.known[eng][key] = val

    def _deps(self, eng, reads, writes):
        mx = {}
        for b in reads:
            for (k, v) in b.wev():
                mx[k] = max(mx.get(k, 0), v)
        for b in writes:
            for (k, v) in b.wev() + b.rev():
                if k == eng and eng == 'pe':
                    continue
                mx[k] = max(mx.get(k, 0), v)
        for k, v in mx.items():
            self._wait(eng, k, v)

    def _mark(self, ev, reads, writes):
        k, v = ev
        for b in reads:
            b.r[k] = max(b.r.get(k, 0), v)
        for b in writes:
            b.w = ev
            b.r = {}
            for p in b.parts.values():
                p.w = None
                p.r = {}

    def op(self, eng, fn, reads=(), writes=(), inc=True):
        self._deps(eng, reads, writes)
        ins = fn(self.e[eng])
        inc = True
        if inc:
            ins.then_inc(self.semobj[eng], 1)
            self.cnt[eng] += 1
            ev = (eng, self.cnt[eng])
        else:
            ev = (eng, self.cnt[eng] + 1)
        self._mark(ev, reads, writes)

    def dma(self, q, out_ap, in_ap, reads=(), writes=()):
        i = self.dq[q]
        self.dq[q] += 1
        key = ('d', q, i % KD)
        val = 16 * (i // KD + 1)
        if i >= KD:
            self._wait(q, key, val - 16)
        self._deps(q, reads, writes)
        self.e[q].dma_start(out=out_ap, in_=in_ap).then_inc(self.semobj[key], 16)
        self._mark((key, val), reads, writes)

    def allgather(self, in_t, out_t, groups, reads, writes):
        i = self.ncc
        self.ncc += 1
        key = ('cc', i % self.NCS)
        val = i // self.NCS + 1
        self._deps('pool', reads, writes)
        self.e['pool'].collective_compute(
            "AllGather", ALU.bypass, replica_groups=groups,
            ins=[in_t.ap().opt()], outs=[out_t.ap().opt()]).then_inc(self.semobj[key])
        self._mark((key, val), reads, writes)

    def barrier(self):
        evs = [(k, self.cnt[k]) for k in ENG if self.cnt[k] > 0]
        for q in ('sp', 'pool'):
            n = self.dq[q]
            for s in range(min(KD, n)):
                last = ((n - 1 - s) // KD) * KD + s
                evs.append((('d', q, s), 16 * (last // KD + 1)))
        for s in range(min(self.NCS, self.ncc)):
            evs.append((('cc', s), (self.ncc - 1 - s) // self.NCS + 1))
        for e in ENG:
            for (k, v) in evs:
                if k != e:
                    self._wait(e, k, v)

    def dram(self, name, shape, dt, **kw):
        t = self.nc.dram_tensor(name, list(shape), dt, **kw)
        return Buf(t, name)


class Phase:
    def __init__(self, k):
        self.k = k
        self.es = contextlib.ExitStack()

    def __enter__(self):
        return self

    def __exit__(self, *a):
        self.k.barrier()
        self.es.close()
        return False

    def sb(self, name, shape, dt):
        self.k.uid += 1
        nm = "%s_%d" % (name, self.k.uid)
        t = self.es.enter_context(self.k.nc.sbuf_tensor(nm, list(shape), dt))
        return Buf(t, nm)

    def pool(self, name, shape, dt, n):
        return Rot([self.sb(name + str(i), shape, dt) for i in range(n)])


class Rot:
    def __init__(self, bufs):
        self.bufs = bufs
        self.i = 0

    def next(self):
        b = self.bufs[self.i % len(self.bufs)]
        self.i += 1
        return b


def const_pack(cfg):
    j = np.arange(128)[:, None]
    i = np.arange(128)[None, :]
    same = (j // 64) == (i // 64)
    ones = np.ones((128, 128), np.float32)
    ident = np.eye(128, dtype=np.float32)
    RT = np.zeros((128, 128), np.float32)
    for m in range(128):
        if m < 64:
            RT[m + 64, m] = -1.0
        else:
            RT[m - 64, m] = 1.0
    tri_f = (same & (j <= i)).astype(np.float32)
    tri_b = (same & (j >= i)).astype(np.float32)
    mid_f = (same & ((j % 64) <= 31)).astype(np.float32)
    mid_b = (same & ((j % 64) >= 32)).astype(np.float32)
    a_f = tri_f - mid_f
    a_b = tri_b - mid_b
    kh_f = (same & (j > i)).astype(np.float32)
    kh_b = (same & (j < i)).astype(np.float32)
    totc = np.zeros((128, 2), np.float32)
    totc[:64, 0] = 1.0
    totc[64:, 1] = 1.0
    pack = np.concatenate([ones, ident, RT, tri_f, tri_b, a_f, a_b, kh_f, kh_b, totc], axis=1)
    return np.ascontiguousarray(pack)


C_ONES, C_ID, C_RT, C_TRIF, C_TRIB, C_AF, C_AB, C_KHF, C_KHB, C_TOTC = [128 * n for n in range(10)]
C_N = 128 * 9 + 2


def cossin_tables(cfg):
    half = 64
    inv = (10000.0 ** (-np.arange(half, dtype=np.float32) / half)).astype(np.float32)
    pos = np.arange(cfg.S, dtype=np.float32)
    ang = (pos[:, None] * inv[None, :]).astype(np.float32)
    cos = np.cos(ang).astype(np.float32).T
    sin = np.sin(ang).astype(np.float32).T
    cs = np.stack([np.concatenate([cos, cos], 0), np.concatenate([sin, sin], 0)], 1)
    return np.ascontiguousarray(cs.astype(np.float32))


def pack_fm(W):
    Kd, N = W.shape
    kc, nt = Kd // 128, N // 128
    return np.ascontiguousarray(W.reshape(kc, 128, nt, 128).transpose(2, 1, 0, 3).reshape(nt * 128, Kd))


def pad_cols(W, n):
    if W.shape[1] == n:
        return W
    out = np.zeros((W.shape[0], n), W.dtype)
    out[:, :W.shape[1]] = W
    return out


def pack_tm(W):
    Kd, N = W.shape
    kc = Kd // 128
    return np.ascontiguousarray(W.reshape(kc, 128, N).transpose(1, 0, 2).reshape(128, kc * N))


def vec_layout(cfg):
    L, KC = cfg.L, cfg.KC
    off = {}
    n = 0
    for nm, sz in [('n1', L * KC), ('nm', L * KC), ('n2', L * KC), ('qn', L), ('kn', L), ('lam', L * 4),
                   ('sub', L * 2), ('gon', L * 2), ('hon', L), ('gbias', L * 2), ('lb', 2 * 2 * L),
                   ('w2', L * 2 * 128), ('sel', 4)]:
        off[nm] = n
        n += sz
    return off, n


def build(cfg):
    D, S, L, F, T, TB, KC, FC = cfg.D, cfg.S, cfg.L, cfg.F, cfg.T, cfg.TB, cfg.KC, cfg.FC
    nc = bass.Bass(target_bir_lowering=False)
    k = K(nc)
    voff, NV = vec_layout(cfg)
    lam_init = [0.8 - 0.6 * math.exp(-0.3 * l) for l in range(L)]

    def ext(name, shape, dt=F32):
        return Buf(nc.dram_tensor(name, list(shape), dt, kind="ExternalInput"), name)

    xT_in = ext("xT", [D, T])
    yT = Buf(nc.dram_tensor("yT", [D, T], F32, kind="ExternalOutput"), "yT")
    kinds = {'f1i': (2 * F, D), 'f1o': (D, F), 'gat': (3 * D, D), 'bra': (D, 2048), 'brg': (D, 1024),
             'brh': (D, 1024), 'wo': (D, D), 'f2i': (2 * F, D), 'f2o': (D, F)}
    wsh = {kn: ext("wsh_" + kn, [L * (N // 4), Kd]) for kn, (N, Kd) in kinds.items()}
    hfm_in = ext("hfm", [L * cfg.NFM * 128, D])
    htm_in = ext("htm", [L * 2 * 128, KC * 512])
    vecs_in = ext("vecs", [128, NV])
    cpack_in = ext("cpack", [128, C_N])
    cs_in = ext("cossin", [128, 2, S])

    xT = k.dram("xTw", [D, T], F32)
    def mtiles(NT, Kd):
        m = max(1, min(NT, cfg.CHB // (32 * Kd * 2)))
        while NT % m:
            m -= 1
        return m
    wmt = {kn: mtiles(N // 128, Kd) for kn, (N, Kd) in kinds.items()}
    wbs = {}
    wfull = {}
    for kn, (N, Kd) in kinds.items():
        m = wmt[kn]
        for l in range(L):
            for j in range(N // 128 // m):
                wbs[kn, l, j] = k.dram("wb_%s%d_%d" % (kn, l, j), [m * 32, Kd], BF16)
                wfull[kn, l, j] = k.dram("wf_%s%d_%d" % (kn, l, j), [4 * m * 32, Kd], BF16)

    def wtile(kn, l, nt, dst):
        m = wmt[kn]
        j, t = nt // m, nt % m
        src_ = wfull[kn, l, j]
        for r2 in range(4):
            k.dma('sp', dst[32 * r2:32 * r2 + 32, :], src_[r2 * m * 32 + t * 32: r2 * m * 32 + t * 32 + 32, :], reads=[src_], writes=[dst])
    hfm = k.dram("hfmb", [L * cfg.NFM * 128, D], BF16)
    htm = k.dram("htmb", [L * 2 * 128, KC * 512], BF16)
    HR = max(128, min(D, (cfg.CHB // (T * 2)) // 128 * 128))
    NHC = D // HR
    hT_own = [k.dram("hT_own%d" % j, [HR, T], BF16) for j in range(NHC)]
    hT_full = [k.dram("hT_full%d" % j, [4 * HR, T], BF16) for j in range(NHC)]
    OR = max(128, min(1024, (cfg.CHB // (T * 2)) // 128 * 128))
    NOP = 1024 // OR
    o_mine = {(s, q): k.dram("o_mine%d_%d" % (s, q), [OR, T], BF16) for s in range(4) for q in range(NOP)}
    o_full = {(s, q): k.dram("o_full%d_%d" % (s, q), [4 * OR, T], BF16) for s in range(4) for q in range(NOP)}

    def o_store(feat0, t0, width, src):
        off = 0
        q, f_ = feat0 // OR, feat0 % OR
        while off < width:
            t = t0 + off
            slab, tl = t // T, t % T
            w = min(width - off, T - tl)
            k.dma('pool', o_mine[slab, q][f_: f_ + 128, tl:tl + w], src[:, off:off + w], reads=[src], writes=[o_mine[slab, q]])
            off += w
    aq = k.dram("aq", [4, 128, S], BF16)
    ak = k.dram("ak", [4, 128, S], BF16)
    av = k.dram("av", [S, 512], BF16)
    gv = k.dram("gv", [S, 512], BF16)
    gqT = k.dram("gqT", [128, S], F32)
    gkT = k.dram("gkT", [128, S], F32)
    ggT = k.dram("ggT", [2, 128, S], F32)
    grT = k.dram("grT", [2, 128, S], F32)
    hqT = k.dram("hqT", [2, 128, S], F32)
    hkT = k.dram("hkT", [4, 128, S], F32)
    hgT = k.dram("hgT", [4, 128, S], F32)
    hoT = k.dram("hoT", [2, 128, S], F32)
    GROUPS4 = [[0, 1, 2, 3], [4, 5, 6, 7]]
    GROUPS8 = [list(range(8))]

    pp = Phase(k)
    cp = pp.sb("cp", [128, C_N], F32)
    cpb = pp.sb("cpb", [128, 3 * 128], BF16)
    vecs = pp.sb("vecs", [128, NV], F32)
    lbv = pp.sb("lbv", [128, 3, 4 * L], F32)
    lamv = pp.sb("lamv", [128, L, 2], F32)
    subg = pp.sb("subg", [128, L * 2], F32)
    w2b = pp.sb("w2b", [16, L * 2 * 128], BF16)
    negb = pp.sb("negb", [128, L * 2], F32)
    epsb = pp.sb("epsb", [128, 2], F32)
    k.op('dve', lambda e: e.memset(epsb[:, 0:1], EPS), writes=[epsb])
    k.op('dve', lambda e: e.memset(epsb[:, 1:2], 1.0), reads=[epsb], writes=[epsb])

    k.dma('sp', cp[:], cpack_in[:, :], writes=[cp])
    k.dma('sp', vecs[:], vecs_in[:, :], writes=[vecs])
    k.op('dve', lambda e: e.tensor_copy(out=cpb[:], in_=cp[:, 0:384]), reads=[cp], writes=[cpb])
    ones_b = lambda: cpb[:, 0:128]
    RT_b = lambda: cpb[:, 256:384]
    ident_f = lambda: cp[:, C_ID:C_ID + 128]

    k.dma('pool', xT[:, :], xT_in[:, :], writes=[xT])
    k.dma('pool', hfm[:, :], hfm_in[:, :], writes=[hfm])
    k.dma('pool', htm[:, :], htm_in[:, :], writes=[htm])
    order = ['f1i', 'f1o', 'gat', 'bra', 'brg', 'brh', 'wo', 'f2i', 'f2o']
    for l in range(L):
        for kn in order:
            N, Kd = kinds[kn]
            rows = N // 4
            m = wmt[kn]
            for j in range(N // 128 // m):
                k.dma('pool', wbs[kn, l, j][:, :], wsh[kn][l * rows + j * m * 32: l * rows + (j + 1) * m * 32, :], writes=[wbs[kn, l, j]])
    for l in range(L):
        for kn in order:
            N, Kd = kinds[kn]
            for j in range(N // 128 // wmt[kn]):
                k.allgather(wbs[kn, l, j].t, wfull[kn, l, j].t, GROUPS4, reads=[wbs[kn, l, j]], writes=[wfull[kn, l, j]])

    V = lambda nm, i, n=1: vecs[:, voff[nm] + i: voff[nm] + i + n]
    with Phase(k) as ph:
        e_ = ph.sb("lbe", [128, 4, L], F32)
        s_ = ph.sb("lbs", [128, 4], F32)
        p_ = ph.sb("lbp", [128, 4, L], F32)
        lb_ = ph.sb("lbl", [128, 4, L], F32)
        raw = vecs[:, voff['lb']: voff['lb'] + 4 * L].rearrange("p (a l) -> p a l", l=L)
        k.op('act', lambda e: e.activation(out=e_[:], in_=raw, func=AF.Exp), reads=[vecs], writes=[e_])
        k.op('dve', lambda e: e.tensor_copy(out=s_[:], in_=e_[:, :, 0]), reads=[e_], writes=[s_])
        for l in range(1, L):
            k.op('dve', lambda e, l=l: e.tensor_tensor(out=s_[:], in0=s_[:], in1=e_[:, :, l], op=ALU.add), reads=[e_, s_], writes=[s_])
        k.op('dve', lambda e: e.reciprocal(out=s_[:], in_=s_[:]), reads=[s_], writes=[s_])
        for l in range(L):
            k.op('dve', lambda e, l=l: e.tensor_tensor(out=p_[:, :, l], in0=e_[:, :, l], in1=s_[:], op=ALU.mult), reads=[e_, s_, p_], writes=[p_])
        k.op('dve', lambda e: e.memset(lb_[:, :, 0], 0.0), reads=[lb_], writes=[lb_])
        for l in range(1, L):
            k.op('dve', lambda e, l=l: e.tensor_tensor(out=lb_[:, :, l], in0=lb_[:, :, l - 1], in1=p_[:, :, l], op=ALU.add), reads=[lb_, p_], writes=[lb_])
        lbf = lb_[:].rearrange("p a l -> p (a l)")
        k.op('dve', lambda e: e.tensor_scalar(out=lbf, in0=lbf, scalar1=0.0, scalar2=1.0 - 1e-6, op0=ALU.max, op1=ALU.min), reads=[lb_], writes=[lb_])
        k.op('dve', lambda e: e.tensor_scalar(out=lbv[:, 0, :], in0=lbf, scalar1=-1.0, scalar2=1.0, op0=ALU.mult, op1=ALU.add), reads=[lb_, lbv], writes=[lbv])
        k.op('dve', lambda e: e.tensor_scalar(out=lbv[:, 1, :], in0=lbf, scalar1=1e-20, scalar2=None, op0=ALU.max), reads=[lb_, lbv], writes=[lbv])
        k.op('dve', lambda e: e.tensor_scalar(out=lbv[:, 2, :], in0=lbv[:, 0, :], scalar1=-1.0, scalar2=None, op0=ALU.mult), reads=[lbv], writes=[lbv])
        pr = ph.sb("lpr", [128, L * 2], BF16)
        le = ph.sb("lle", [128, L * 2], F32)
        for l in range(L):
            for m in range(2):
                k.op('dve', lambda e, l=l, m=m: e.tensor_tensor(out=pr[:, l * 2 + m: l * 2 + m + 1], in0=V('lam', l * 4 + 2 * m), in1=V('lam', l * 4 + 2 * m + 1), op=ALU.mult), reads=[vecs, pr], writes=[pr])
        pz = k.psum()
        k.op('pe', lambda e: e.matmul(pz[:, 0:L * 2], lhsT=ones_b(), rhs=pr[:], start=True, stop=True), reads=[cpb, pr], writes=[pz])
        k.op('act', lambda e: e.activation(out=le[:], in_=pz[:, 0:L * 2], func=AF.Exp), reads=[pz], writes=[le])
        for l in range(L):
            k.op('dve', lambda e, l=l: e.scalar_tensor_tensor(out=lamv[:, l, 0:1], in0=le[:, 2 * l + 1: 2 * l + 2], scalar=-lam_init[l], in1=le[:, 2 * l: 2 * l + 1], op0=ALU.add, op1=ALU.subtract), reads=[le, lamv], writes=[lamv])
            k.op('dve', lambda e, l=l: e.tensor_scalar(out=subg[:, 2 * l: 2 * l + 2], in0=V('sub', 2 * l, 2), scalar1=1.0 - lam_init[l], scalar2=None, op0=ALU.mult), reads=[vecs, subg], writes=[subg])
        k.op('dve', lambda e: e.tensor_copy(out=w2b[:], in_=vecs[0:16, voff['w2']: voff['w2'] + L * 2 * 128]), reads=[vecs], writes=[w2b])
        k.op('dve', lambda e: e.tensor_scalar(out=negb[:], in0=V('gbias', 0, 2 * L), scalar1=-1.0, scalar2=None, op0=ALU.mult), reads=[vecs], writes=[negb])

    def rstd_from_ps(ph, ps_ap, ps_buf, n, width, rpool):
        r = rpool.next()
        k.op('act', lambda e: e.activation(out=r[:, 0:width], in_=ps_ap, func=AF.Ln, bias=epsb[:, 0:1], scale=1.0 / n), reads=[ps_buf, epsb], writes=[r])
        k.op('act', lambda e: e.activation(out=r[:, 0:width], in_=r[:, 0:width], func=AF.Exp, scale=-0.5), reads=[r], writes=[r])
        return r

    def norm_block(ph, xsrc, t0, gname, l, hblk, xpool, sqpool, rpool):
        pss = k.psum()
        for c in range(KC):
            xc = xpool.next()
            k.dma('sp', xc[:], xsrc[c * 128:(c + 1) * 128, t0:t0 + TB], reads=[xsrc.part(c)], writes=[xc])
            sq = sqpool.next()
            k.op('act', lambda e: e.activation(out=sq[:], in_=xc[:], func=AF.Square), reads=[xc], writes=[sq])
            k.op('pe', lambda e, c=c: e.matmul(pss[:, 0:TB], lhsT=ones_b(), rhs=sq[:], start=(c == 0), stop=(c == KC - 1)), reads=[cpb, sq], writes=[pss], inc=(c == KC - 1))
        r = rstd_from_ps(ph, pss[:, 0:TB], pss, D, TB, rpool)
        for c in range(KC):
            xc = xpool.next()
            k.dma('sp', xc[:], xsrc[c * 128:(c + 1) * 128, t0:t0 + TB], reads=[xsrc.part(c)], writes=[xc])
            k.op('dve', lambda e, c=c: e.scalar_tensor_tensor(out=hblk[:, c, :], in0=xc[:], scalar=V(gname, l * KC + c), in1=r[:, 0:TB], op0=ALU.mult, op1=ALU.mult), reads=[xc, r, vecs], writes=[hblk.part(c)])

    def ffn(l, which, xsrc, xdst):
        kin, kout = 'f%di' % which, 'f%do' % which
        gname = 'n%d' % which
        with Phase(k) as ph:
            hblk = ph.sb("hblk", [128, KC, TB], BF16)
            hid = ph.sb("hid", [128, FC, TB], BF16)
            xpool = ph.pool("xc", [128, TB], F32, 3)
            sqpool = ph.pool("sq", [128, TB], BF16, 2)
            rpool = ph.pool("rs", [128, TB], F32, 1)
            wpool = ph.pool("wt", [128, D], BF16, 4)
            wopool = ph.pool("wo", [128, F], BF16, 2)
            sgpool = ph.pool("sg", [128, TB], F32, 2)
            for tb in range(cfg.NTB):
                t0 = tb * TB
                norm_block(ph, xsrc, t0, gname, l, hblk, xpool, sqpool, rpool)
                for j in range(FC):
                    wg, wu = wpool.next(), wpool.next()
                    wtile(kin, l, j, wg)
                    wtile(kin, l, FC + j, wu)
                    pg, pu = k.psum(), k.psum()
                    for c in range(KC):
                        k.op('pe', lambda e, c=c: e.matmul(pg[:, 0:TB], lhsT=wg[:, c * 128:(c + 1) * 128], rhs=hblk[:, c, :], start=(c == 0), stop=(c == KC - 1)), reads=[wg, hblk.part(c)], writes=[pg], inc=(c == KC - 1))
                    for c in range(KC):
                        k.op('pe', lambda e, c=c: e.matmul(pu[:, 0:TB], lhsT=wu[:, c * 128:(c + 1) * 128], rhs=hblk[:, c, :], start=(c == 0), stop=(c == KC - 1)), reads=[wu, hblk.part(c)], writes=[pu], inc=(c == KC - 1))
                    sg = sgpool.next()
                    k.op('act', lambda e: e.activation(out=sg[:], in_=pg[:, 0:TB], func=AF.Silu), reads=[pg], writes=[sg])
                    k.op('dve', lambda e, j=j: e.tensor_tensor(out=hid[:, j, :], in0=sg[:], in1=pu[:, 0:TB], op=ALU.mult), reads=[sg, pu], writes=[hid.part(j)])
                for i in range(KC):
                    wo_ = wopool.next()
                    wtile(kout, l, i, wo_)
                    po = k.psum()
                    for fc in range(FC):
                        k.op('pe', lambda e, fc=fc: e.matmul(po[:, 0:TB], lhsT=wo_[:, fc * 128:(fc + 1) * 128], rhs=hid[:, fc, :], start=(fc == 0), stop=(fc == FC - 1)), reads=[wo_, hid.part(fc)], writes=[po], inc=(fc == FC - 1))
                    xc = xpool.next()
                    k.dma('sp', xc[:], xsrc[i * 128:(i + 1) * 128, t0:t0 + TB], reads=[xsrc.part(i)], writes=[xc])
                    k.op('dve', lambda e: e.scalar_tensor_tensor(out=xc[:], in0=po[:, 0:TB], scalar=0.5, in1=xc[:], op0=ALU.mult, op1=ALU.add), reads=[po, xc], writes=[xc])
                    k.dma('pool', xdst[i * 128:(i + 1) * 128, t0:t0 + TB], xc[:], reads=[xc], writes=[xdst.part(i)])

    def mixnorm(l):
        with Phase(k) as ph:
            hblk = ph.sb("hblk", [128, KC, TB], BF16)
            xpool = ph.pool("xc", [128, TB], F32, 3)
            sqpool = ph.pool("sq", [128, TB], BF16, 2)
            rpool = ph.pool("rs", [128, TB], F32, 1)
            for tb in range(cfg.NTB):
                t0 = tb * TB
                norm_block(ph, xT, t0, 'nm', l, hblk, xpool, sqpool, rpool)
                for j in range(NHC):
                    k.dma('pool', hT_own[j][:, t0:t0 + TB].rearrange("(c p) t -> p c t", p=128), hblk[:, j * (HR // 128):(j + 1) * (HR // 128), :], reads=[hblk], writes=[hT_own[j]])
        for j in range(NHC):
            k.allgather(hT_own[j].t, hT_full[j].t, GROUPS4, reads=[hT_own[j]], writes=[hT_full[j]])

    def projection(l):
        with Phase(k) as ph:
            hblk = ph.sb("hblk", [128, KC, TB], BF16)
            wpool = ph.pool("wt", [128, D], BF16, 3)
            tmw = ph.pool("tmw", [128, KC, 512], BF16, 2)
            cs = ph.sb("cs", [128, 2, TB], F32)
            sqpool = ph.pool("sq", [128, TB], BF16, 2)
            rpool = ph.pool("rs", [128, TB], F32, 2)
            st32 = ph.pool("st32", [128, TB], F32, 4)
            stb = ph.pool("stb", [128, TB], BF16, 4)
            tmp = ph.pool("tmp", [128, TB], F32, 4)
            vst = ph.pool("vst", [128, 512], BF16, 3)
            lr16 = ph.pool("lr16", [16, TB], BF16, 2)
            base = l * cfg.NFM * 128
            for sb_ in range(cfg.NSB):
                t0 = sb_ * TB
                slab = t0 // T
                tl = t0 - slab * T
                for j in range(NHC):
                    k.dma('sp', hblk[:, j * (HR // 128):(j + 1) * (HR // 128), :], hT_full[j][slab * HR:(slab + 1) * HR, tl:tl + TB].rearrange("(c p) t -> p c t", p=128), reads=[hT_full[j]], writes=[hblk])
                k.dma('sp', cs[:], cs_in[:, :, t0:t0 + TB], writes=[cs])

                def fm_tile(ti):
                    w = wpool.next()
                    k.dma('sp', w[:], hfm[base + ti * 128: base + (ti + 1) * 128, :], reads=[hfm], writes=[w])
                    p = k.psum()
                    for c in range(KC):
                        k.op('pe', lambda e, c=c: e.matmul(p[:, 0:TB], lhsT=w[:, c * 128:(c + 1) * 128], rhs=hblk[:, c, :], start=(c == 0), stop=(c == KC - 1)), reads=[w, hblk], writes=[p], inc=(c == KC - 1))
                    return p

                def store32(p_ap, pbuf, dst_ap, dstbuf, scale=None):
                    s = st32.next()
                    if scale is None:
                        k.op('act', lambda e: e.activation(out=s[:], in_=p_ap, func=AF.Copy), reads=[pbuf], writes=[s])
                    else:
                        k.op('act', lambda e: e.activation(out=s[:], in_=p_ap, func=AF.Copy, scale=scale), reads=[pbuf], writes=[s])
                    k.dma('pool', dst_ap, s[:], reads=[s], writes=[dstbuf])

                for ti in range(8):
                    p = fm_tile(ti)
                    gn = 'qn' if ti < 4 else 'kn'
                    sq = sqpool.next()
                    k.op('act', lambda e: e.activation(out=sq[:], in_=p[:, 0:TB], func=AF.Square), reads=[p], writes=[sq])
                    p2 = k.psum()
                    k.op('pe', lambda e: e.matmul(p2[:, 0:TB], lhsT=ones_b(), rhs=sq[:], start=True, stop=True), reads=[cpb, sq], writes=[p2])
                    r = rstd_from_ps(ph, p2[:, 0:TB], p2, 128, TB, rpool)
                    qn = tmp.next()
                    k.op('dve', lambda e: e.scalar_tensor_tensor(out=qn[:], in0=p[:, 0:TB], scalar=V(gn, l), in1=r[:, 0:TB], op0=ALU.mult, op1=ALU.mult), reads=[p, r, vecs], writes=[qn])
                    qb = stb.next()
                    k.op('act', lambda e: e.activation(out=qb[:], in_=qn[:], func=AF.Copy), reads=[qn], writes=[qb])
                    p3 = k.psum()
                    k.op('pe', lambda e: e.matmul(p3[:, 0:TB], lhsT=RT_b(), rhs=qb[:], start=True, stop=True), reads=[cpb, qb], writes=[p3])
                    t1 = tmp.next()
                    k.op('dve', lambda e: e.tensor_tensor(out=t1[:], in0=qn[:], in1=cs[:, 0, :], op=ALU.mult), reads=[qn, cs], writes=[t1])
                    t2 = tmp.next()
                    k.op('dve', lambda e: e.tensor_tensor(out=t2[:], in0=p3[:, 0:TB], in1=cs[:, 1, :], op=ALU.mult), reads=[p3, cs], writes=[t2])
                    ob = stb.next()
                    k.op('dve', lambda e: e.tensor_tensor(out=ob[:], in0=t1[:], in1=t2[:], op=ALU.add), reads=[t1, t2], writes=[ob])
                    dst = aq if ti < 4 else ak
                    k.dma('pool', dst[ti % 4, :, t0:t0 + TB], ob[:], reads=[ob], writes=[dst])
                p = fm_tile(8)
                store32(p[:, 0:TB], p, gqT[:, t0:t0 + TB], gqT, scale=128 ** -0.5)
                p = fm_tile(9)
                store32(p[:, 0:TB], p, gkT[:, t0:t0 + TB], gkT)
                for j in range(2):
                    p = fm_tile(10 + j)
                    store32(p[:, 0:TB], p, grT[j, :, t0:t0 + TB], grT)
                for d in range(2):
                    p = fm_tile(12 + d)
                    lr = lr16.next()
                    k.op('act', lambda e: e.activation(out=lr[:], in_=p[0:16, 0:TB], func=AF.Copy), reads=[p], writes=[lr])
                    p2 = k.psum()
                    k.op('pe', lambda e, d=d: e.matmul(p2[:, 0:TB], lhsT=w2b[:, (l * 2 + d) * 128:(l * 2 + d + 1) * 128], rhs=lr[:], start=True, stop=True), reads=[w2b, lr], writes=[p2])
                    e1 = tmp.next()
                    k.op('act', lambda e, d=d: e.activation(out=e1[:], in_=p2[:, 0:TB], func=AF.Exp, bias=negb[:, l * 2 + d: l * 2 + d + 1], scale=-1.0), reads=[p2, negb], writes=[e1])
                    k.op('act', lambda e: e.activation(out=e1[:], in_=e1[:], func=AF.Ln, bias=epsb[:, 1:2], scale=1.0), reads=[e1], writes=[e1])
                    s = st32.next()
                    k.op('dve', lambda e: e.tensor_scalar(out=s[:], in0=e1[:], scalar1=-1.0 / 16.0, scalar2=None, op0=ALU.mult), reads=[e1], writes=[s])
                    k.dma('pool', ggT[d, :, t0:t0 + TB], s[:], reads=[s], writes=[ggT])
                for h in range(2):
                    p = fm_tile(14 + h)
                    store32(p[:, 0:TB], p, hqT[h, :, t0:t0 + TB], hqT, scale=128 ** -0.5)
                for d in range(2):
                    for h in range(2):
                        p = fm_tile(16 + d * 2 + h)
                        li = (d * 2 + h) * L + l
                        e1 = tmp.next()
                        k.op('act', lambda e: e.activation(out=e1[:], in_=p[:, 0:TB], func=AF.Exp, scale=-1.0), reads=[p], writes=[e1])
                        k.op('dve', lambda e: e.tensor_scalar(out=e1[:], in0=e1[:], scalar1=1.0, scalar2=None, op0=ALU.add), reads=[e1], writes=[e1])
                        sg = tmp.next()
                        k.op('dve', lambda e: e.reciprocal(out=sg[:], in_=e1[:]), reads=[e1], writes=[sg])
                        f_ = tmp.next()
                        k.op('dve', lambda e, li=li: e.tensor_scalar(out=f_[:], in0=sg[:], scalar1=lbv[:, 0, li:li + 1], scalar2=lbv[:, 1, li:li + 1], op0=ALU.mult, op1=ALU.add), reads=[sg, lbv], writes=[f_])
                        s = st32.next()
                        k.op('act', lambda e: e.activation(out=s[:], in_=f_[:], func=AF.Ln), reads=[f_], writes=[s])
                        k.dma('pool', hgT[d * 2 + h, :, t0:t0 + TB], s[:], reads=[s], writes=[hgT])
                        s2 = st32.next()
                        k.op('dve', lambda e, li=li: e.tensor_scalar(out=s2[:], in0=sg[:], scalar1=lbv[:, 2, li:li + 1], scalar2=lbv[:, 0, li:li + 1], op0=ALU.mult, op1=ALU.add), reads=[sg, lbv], writes=[s2])
                        k.dma('pool', hkT[d * 2 + h, :, t0:t0 + TB], s2[:], reads=[s2], writes=[hkT])
                for h in range(2):
                    p = fm_tile(20 + h)
                    store32(p[:, 0:TB], p, hoT[h, :, t0:t0 + TB], hoT)
                for g in range(2):
                    w = tmw.next()
                    k.dma('sp', w[:], htm[(l * 2 + g) * 128:(l * 2 + g + 1) * 128, :].rearrange("p (c n) -> p c n", n=512), reads=[htm], writes=[w])
                    for sub in range(TB // 128):
                        p = k.psum()
                        for c in range(KC):
                            k.op('pe', lambda e, c=c, sub=sub: e.matmul(p[:, 0:512], lhsT=hblk[:, c, sub * 128:(sub + 1) * 128], rhs=w[:, c, :], start=(c == 0), stop=(c == KC - 1)), reads=[w, hblk], writes=[p], inc=(c == KC - 1))
                        vs = vst.next()
                        k.op('act', lambda e: e.activation(out=vs[:], in_=p[:, 0:512], func=AF.Copy), reads=[p], writes=[vs])
                        dst = av if g == 0 else gv
                        k.dma('pool', dst[t0 + sub * 128: t0 + (sub + 1) * 128, :], vs[:], reads=[vs], writes=[dst])

    def attention(l):
        QG = cfg.QG
        NK = S // 128
        with Phase(k) as ph:
            vt = ph.sb("vt", [128, NK, 256], BF16)
            ppool = ph.pool("pT", [128, QG], BF16, 4)
            o0n = ph.sb("o0n", [128, 2, QG], F32)
            od = ph.sb("od", [128, 2, QG], F32)
            rr = ph.pool("rr", [128, QG], F32, 2)
            tt_ = ph.pool("tt", [128, QG], F32, 2)
            sqb = ph.sb("sqb", [128, 2, QG], BF16)
            rpool = ph.pool("rs", [128, QG], F32, 1)
            ost = ph.pool("ost", [128, QG], BF16, 3)
            scale = 128 ** -0.5
            kq = {}
            for m in range(2):
                kq[m] = (ph.sb("kTm%d" % m, [128, S], BF16), ph.sb("qTm%d" % m, [128, S], BF16))
            accsets = Rot([[k.ps[0], k.ps[1], k.ps[2]], [k.ps[3], k.ps[4], k.ps[5]]])
            pspool = Rot([k.ps[6], k.ps[7]])
            for hh in range(2):
                k.dma('sp', vt[:], av[:, hh * 256:(hh + 1) * 256].rearrange("(c p) e -> p c e", p=128), reads=[av], writes=[vt])
                for m in range(2):
                    k.dma('sp', kq[m][0][:], ak[hh * 2 + m, :, :], reads=[ak], writes=[kq[m][0]])
                    k.dma('sp', kq[m][1][:], aq[hh * 2 + m, :, :], reads=[aq], writes=[kq[m][1]])
                for g in range(S // QG):
                    q0 = g * QG
                    for m in range(2):
                        kTm, qTm = kq[m]
                        acc0, acc1, accs = accsets.next()
                        for kc in range(NK):
                            pss = pspool.next()
                            k.op('pe', lambda e, kc=kc: e.matmul(pss[:, 0:QG], lhsT=kTm[:, kc * 128:(kc + 1) * 128], rhs=qTm[:, q0:q0 + QG], start=True, stop=True), reads=[kTm, qTm], writes=[pss])
                            pT = ppool.next()
                            k.op('act', lambda e: e.activation(out=pT[:], in_=pss[:, 0:QG], func=AF.Exp, scale=scale), reads=[pss], writes=[pT])
                            last = (kc == NK - 1)
                            k.op('pe', lambda e, kc=kc: e.matmul(acc0[:, 0:QG], lhsT=vt[:, kc, 0:128], rhs=pT[:], start=(kc == 0), stop=last), reads=[vt, pT], writes=[acc0], inc=False)
                            k.op('pe', lambda e, kc=kc: e.matmul(acc1[:, 0:QG], lhsT=vt[:, kc, 128:256], rhs=pT[:], start=(kc == 0), stop=last), reads=[vt, pT], writes=[acc1], inc=False)
                            k.op('pe', lambda e, kc=kc: e.matmul(accs[:, 0:QG], lhsT=ones_b(), rhs=pT[:], start=(kc == 0), stop=last), reads=[cpb, pT], writes=[accs], inc=True)
                        r = rr.next()
                        k.op('dve', lambda e: e.reciprocal(out=r[:], in_=accs[:, 0:QG]), reads=[accs], writes=[r])
                        if m == 0:
                            k.op('dve', lambda e: e.tensor_tensor(out=o0n[:, 0, :], in0=acc0[:, 0:QG], in1=r[:], op=ALU.mult), reads=[acc0, r, o0n], writes=[o0n])
                            k.op('dve', lambda e: e.tensor_tensor(out=o0n[:, 1, :], in0=acc1[:, 0:QG], in1=r[:], op=ALU.mult), reads=[acc1, r, o0n], writes=[o0n])
                        else:
                            for et, acc in ((0, acc0), (1, acc1)):
                                t_ = tt_.next()
                                k.op('dve', lambda e, acc=acc: e.tensor_tensor(out=t_[:], in0=acc[:, 0:QG], in1=r[:], op=ALU.mult), reads=[acc, r], writes=[t_])
                                k.op('dve', lambda e, et=et: e.scalar_tensor_tensor(out=od[:, et, :], in0=t_[:], scalar=lamv[:, l, 0:1], in1=o0n[:, et, :], op0=ALU.mult, op1=ALU.add), reads=[t_, lamv, o0n, od], writes=[od])
                    k.op('act', lambda e: e.activation(out=sqb[:], in_=od[:], func=AF.Square), reads=[od], writes=[sqb])
                    p2 = pspool.next()
                    for et in range(2):
                        k.op('pe', lambda e, et=et: e.matmul(p2[:, 0:QG], lhsT=ones_b(), rhs=sqb[:, et, :], start=(et == 0), stop=(et == 1)), reads=[cpb, sqb], writes=[p2], inc=(et == 1))
                    r = rstd_from_ps(ph, p2[:, 0:QG], p2, 256, QG, rpool)
                    for et in range(2):
                        ob = ost.next()
                        k.op('dve', lambda e, et=et: e.scalar_tensor_tensor(out=ob[:], in0=od[:, et, :], scalar=subg[:, 2 * l + et: 2 * l + et + 1], in1=r[:, 0:QG], op0=ALU.mult, op1=ALU.mult), reads=[od, subg, r], writes=[ob])
                        o_store(hh * 256 + et * 128, q0, QG, ob)

    def scans(l):
        NT = S // 128
        with Phase(k) as ph:
            chains = []
            osum = {}
            hd = [dict(name='g', q=lambda: gqT[:, :], dv=256, vcol=0, orow=512, gate=lambda j: grT[j], gain=('gon', 2 * l)),
                  dict(name='h0', q=lambda: hqT[0], dv=128, vcol=256, orow=768, gate=lambda j: hoT[0], gain=('hon', l)),
                  dict(name='h1', q=lambda: hqT[1], dv=128, vcol=384, orow=896, gate=lambda j: hoT[1], gain=('hon', l))]
            for hi_, h in enumerate(hd):
                osum[hi_] = ph.sb("osum" + h['name'], [128, h['dv'] // 128, S], F32)
                for d in range(2):
                    if hi_ == 0:
                        ksrc, gsrc = gkT[:, :], ggT[d]
                        kbuf, gbuf, qbuf = gkT, ggT, gqT
                    else:
                        ksrc, gsrc = hkT[d * 2 + hi_ - 1], hgT[d * 2 + hi_ - 1]
                        kbuf, gbuf, qbuf = hkT, hgT, hqT
                    Sf = ph.sb("S%d%d" % (hi_, d), [128, h['dv']], F32)
                    Sb = ph.sb("Sb%d%d" % (hi_, d), [128, h['dv']], BF16)
                    k.op('dve', lambda e, Sf=Sf: e.memset(Sf[:], 0.0), writes=[Sf])
                    k.op('dve', lambda e, Sb=Sb: e.memset(Sb[:], 0.0), writes=[Sb])
                    chains.append(dict(h=hi_, d=d, ksrc=ksrc, gsrc=gsrc, kbuf=kbuf, gbuf=gbuf, qbuf=qbuf, Sf=Sf, Sb=Sb))
            ld = ph.pool("ld", [128, 3, 128], F32, 3)
            vld = ph.pool("vld", [128, 256], BF16, 3)
            tk = ph.pool("tk", [128, 2, 128], F32, 3)
            ex = ph.pool("ex", [128, 4, 128], F32, 2)
            dec = ph.pool("dec", [128, 2], F32, 3)
            qk = ph.pool("qk", [128, 4, 128], BF16, 3)
            scm = ph.pool("scm", [128, 128], BF16, 3)
            written = {}
            for step in range(NT):
                for ch in chains:
                    h = hd[ch['h']]
                    d = ch['d']
                    dv = h['dv']
                    tt = step if d == 0 else NT - 1 - step
                    t0 = tt * 128
                    L_ = ld.next()
                    k.dma('sp', L_[:, 0, :], ch['gsrc'][:, t0:t0 + 128], reads=[ch['gbuf']], writes=[L_])
                    k.dma('sp', L_[:, 1, :], ch['ksrc'][:, t0:t0 + 128], reads=[ch['kbuf']], writes=[L_])
                    k.dma('sp', L_[:, 2, :], h['q']()[:, t0:t0 + 128], reads=[ch['qbuf']], writes=[L_])
                    vt_ = vld.next()
                    k.dma('sp', vt_[:, 0:dv], gv[t0:t0 + 128, h['vcol']:h['vcol'] + dv], reads=[gv], writes=[vt_])
                    ptr = k.psum()
                    k.op('pe', lambda e: e.transpose(ptr[:, 0:128], L_[:, 0, :], ident_f()), reads=[L_, cp], writes=[ptr], inc=False)
                    k.op('pe', lambda e: e.transpose(ptr[:, 128:256], L_[:, 1, :], ident_f()), reads=[L_, cp], writes=[ptr])
                    tk_ = tk.next()
                    k.op('act', lambda e: e.activation(out=tk_[:].rearrange("p a b -> p (a b)"), in_=ptr[:, 0:256], func=AF.Copy), reads=[ptr], writes=[tk_])
                    CA, CT, CK = (C_AF, C_TRIF, C_KHF) if d == 0 else (C_AB, C_TRIB, C_KHB)
                    pm = k.psum()
                    k.op('pe', lambda e: e.matmul(pm[:, 0:128], lhsT=tk_[:, 0, :], rhs=cp[:, CA:CA + 128], start=True, stop=True), reads=[tk_, cp], writes=[pm], inc=False)
                    k.op('pe', lambda e: e.matmul(pm[:, 128:256], lhsT=tk_[:, 0, :], rhs=cp[:, CT:CT + 128], start=True, stop=True), reads=[tk_, cp], writes=[pm], inc=False)
                    k.op('pe', lambda e: e.matmul(pm[:, 256:384], lhsT=cp[:, CK:CK + 128], rhs=tk_[:, 0, :], start=True, stop=True), reads=[tk_, cp], writes=[pm], inc=False)
                    k.op('pe', lambda e: e.matmul(pm[:, 384:386], lhsT=tk_[:, 0, :], rhs=cp[:, C_TOTC:C_TOTC + 2], start=True, stop=True), reads=[tk_, cp], writes=[pm])
                    ex_ = ex.next()
                    k.op('act', lambda e: e.activation(out=ex_[:, 0, :], in_=pm[:, 0:128], func=AF.Exp), reads=[pm], writes=[ex_])
                    k.op('act', lambda e: e.activation(out=ex_[:, 1, :], in_=pm[:, 0:128], func=AF.Exp, scale=-1.0), reads=[pm, ex_], writes=[ex_])
                    k.op('act', lambda e: e.activation(out=ex_[:, 2:4, :].rearrange("p a b -> p (a b)"), in_=pm[:, 128:384], func=AF.Exp), reads=[pm, ex_], writes=[ex_])
                    dc = dec.next()
                    k.op('act', lambda e: e.activation(out=dc[:], in_=pm[:, 384:386], func=AF.Exp), reads=[pm], writes=[dc])
                    qk_ = qk.next()
                    k.op('dve', lambda e: e.tensor_tensor(out=qk_[:, 0, :], in0=L_[:, 2, :], in1=ex_[:, 0, :], op=ALU.mult), reads=[L_, ex_], writes=[qk_])
                    k.op('dve', lambda e: e.tensor_tensor(out=qk_[:, 1, :], in0=L_[:, 1, :], in1=ex_[:, 1, :], op=ALU.mult), reads=[L_, ex_, qk_], writes=[qk_])
                    k.op('dve', lambda e: e.tensor_tensor(out=qk_[:, 2, :], in0=L_[:, 2, :], in1=ex_[:, 2, :], op=ALU.mult), reads=[L_, ex_, qk_], writes=[qk_])
                    k.op('dve', lambda e: e.tensor_tensor(out=qk_[:, 3, :], in0=tk_[:, 1, :], in1=ex_[:, 3, :], op=ALU.mult), reads=[tk_, ex_, qk_], writes=[qk_])
                    psc = k.psum()
                    k.op('pe', lambda e: e.matmul(psc[:, 0:128], lhsT=qk_[:, 1, :], rhs=qk_[:, 0, :], start=True, stop=True), reads=[qk_], writes=[psc])
                    sm = scm.next()
                    k.op('dve', lambda e: e.tensor_tensor(out=sm[:], in0=psc[:, 0:128], in1=cp[:, CT:CT + 128], op=ALU.mult), reads=[psc, cp], writes=[sm])
                    nvt = dv // 128
                    pos = [k.psum() for _ in range(nvt)]
                    corder = [0, 1] if d == 0 else [1, 0]
                    Sf, Sb = ch['Sf'], ch['Sb']
                    for vi in range(nvt):
                        k.op('pe', lambda e, vi=vi: e.matmul(pos[vi][:, 0:128], lhsT=vt_[:, vi * 128:(vi + 1) * 128], rhs=sm[:], start=True, stop=False), reads=[vt_, sm], writes=[pos[vi]], inc=False)
                    for ci, c_ in enumerate(corder):
                        cs_ = slice(c_ * 64, c_ * 64 + 64)
                        for vi in range(nvt):
                            k.op('pe', lambda e, vi=vi, cs_=cs_: e.matmul(pos[vi][:, cs_], lhsT=Sb[:, vi * 128:(vi + 1) * 128], rhs=qk_[:, 2, cs_], start=False, stop=(ci == 1)), reads=[Sb, qk_], writes=[pos[vi]], inc=(ci == 1 and vi == nvt - 1))
                        pd = k.psum()
                        k.op('pe', lambda e, cs_=cs_: e.matmul(pd[:, 0:dv], lhsT=qk_[cs_, 3, :], rhs=vt_[cs_, 0:dv], start=True, stop=True), reads=[qk_, vt_], writes=[pd])
                        k.op('dve', lambda e, c_=c_: e.scalar_tensor_tensor(out=Sf[:], in0=Sf[:], scalar=dc[:, c_:c_ + 1], in1=pd[:, 0:dv], op0=ALU.mult, op1=ALU.add), reads=[Sf, dc, pd], writes=[Sf])
                        k.op('act', lambda e: e.activation(out=Sb[:], in_=Sf[:], func=AF.Copy), reads=[Sf], writes=[Sb])
                    os_ = osum[ch['h']]
                    key = (ch['h'], tt)
                    for vi in range(nvt):
                        if key not in written:
                            k.op('act', lambda e, vi=vi: e.activation(out=os_[:, vi, t0:t0 + 128], in_=pos[vi][:, 0:128], func=AF.Copy), reads=[pos[vi]], writes=[os_.part(tt)])
                        else:
                            k.op('dve', lambda e, vi=vi: e.tensor_tensor(out=os_[:, vi, t0:t0 + 128], in0=os_[:, vi, t0:t0 + 128], in1=pos[vi][:, 0:128], op=ALU.add), reads=[pos[vi], os_.part(tt)], writes=[os_.part(tt)])
                    written[key] = True
            sqp = ph.pool("sq", [128, TB], BF16, 2)
            rpool = ph.pool("rs", [128, TB], F32, 1)
            gp = ph.pool("gp", [128, TB], F32, 3)
            ost = ph.pool("ost", [128, TB], BF16, 3)
            for hi_, h in enumerate(hd):
                nvt = h['dv'] // 128
                os_ = osum[hi_]
                for sb_ in range(S // TB):
                    t0 = sb_ * TB
                    p2 = k.psum()
                    for vi in range(nvt):
                        sq = sqp.next()
                        k.op('act', lambda e, vi=vi: e.activation(out=sq[:], in_=os_[:, vi, t0:t0 + TB], func=AF.Square), reads=[os_], writes=[sq])
                        k.op('pe', lambda e, vi=vi: e.matmul(p2[:, 0:TB], lhsT=ones_b(), rhs=sq[:], start=(vi == 0), stop=(vi == nvt - 1)), reads=[cpb, sq], writes=[p2], inc=(vi == nvt - 1))
                    r = rstd_from_ps(ph, p2[:, 0:TB], p2, h['dv'], TB, rpool)
                    for vi in range(nvt):
                        g_ = gp.next()
                        gsrc = h['gate'](vi)
                        gb = grT if hi_ == 0 else hoT
                        k.dma('sp', g_[:], gsrc[:, t0:t0 + TB], reads=[gb], writes=[g_])
                        k.op('act', lambda e: e.activation(out=g_[:], in_=g_[:], func=AF.Silu), reads=[g_], writes=[g_])
                        t_ = gp.next()
                        gn, gi = h['gain']
                        k.op('dve', lambda e, vi=vi: e.scalar_tensor_tensor(out=t_[:], in0=os_[:, vi, t0:t0 + TB], scalar=V(gn, gi + vi), in1=r[:, 0:TB], op0=ALU.mult, op1=ALU.mult), reads=[os_, vecs, r], writes=[t_])
                        ob = ost.next()
                        k.op('dve', lambda e: e.tensor_tensor(out=ob[:], in0=t_[:], in1=g_[:], op=ALU.mult), reads=[t_, g_], writes=[ob])
                        o_store(h['orow'] + vi * 128, t0, TB, ob)


    def merge(l):
        for key_ in sorted(o_mine):
            k.allgather(o_mine[key_].t, o_full[key_].t, GROUPS4, reads=[o_mine[key_]], writes=[o_full[key_]])
        with Phase(k) as ph:
            hblk = ph.sb("hblk", [128, KC, TB], BF16)
            ob = ph.sb("ob", [128, 32, TB], BF16)
            mg = ph.sb("mg", [128, KC, TB], BF16)
            stg = ph.pool("stg", [128, 8, TB], BF16, 3)
            wpool = ph.pool("wt", [128, D], BF16, 4)
            bpool = ph.pool("bt", [128, 2048], BF16, 2)
            bpool2 = ph.pool("bt2", [128, 1024], BF16, 3)
            sgp = ph.pool("sg", [128, TB], F32, 4)
            mp = ph.pool("mp", [128, TB], F32, 3)
            xpool = ph.pool("xc", [128, TB], F32, 3)
            for tb in range(cfg.NTB):
                t0 = tb * TB
                for j in range(NHC):
                    k.dma('sp', hblk[:, j * (HR // 128):(j + 1) * (HR // 128), :], hT_own[j][:, t0:t0 + TB].rearrange("(c p) t -> p c t", p=128), reads=[hT_own[j]], writes=[hblk])
                for s in range(4):
                    for r2 in range(4):
                        st = stg.next()
                        for q in range(NOP):
                            k.dma('sp', st[:, q * (OR // 128):(q + 1) * (OR // 128), :], o_full[s, q][r2 * OR:(r2 + 1) * OR, t0:t0 + TB].rearrange("(c p) t -> p c t", p=128), reads=[o_full[s, q]], writes=[st])
                        dst = ob[:, r2 * 8:(r2 + 1) * 8, :]
                        if s == 0:
                            k.op('dve', lambda e: e.tensor_scalar(out=dst, in0=st[:], scalar1=V('sel', s), scalar2=None, op0=ALU.mult), reads=[st, vecs], writes=[ob.part(r2)])
                        else:
                            k.op('dve', lambda e: e.scalar_tensor_tensor(out=dst, in0=st[:], scalar=V('sel', s), in1=dst, op0=ALU.mult, op1=ALU.add), reads=[st, vecs, ob.part(r2)], writes=[ob.part(r2)])
                for i in range(KC):
                    ws = []
                    for gi in range(3):
                        w = wpool.next()
                        wtile('gat', l, gi * KC + i, w)
                        ws.append(w)
                    ba = bpool.next()
                    wtile('bra', l, i, ba)
                    bg, bh = bpool2.next(), bpool2.next()
                    wtile('brg', l, i, bg)
                    wtile('brh', l, i, bh)
                    pgs = []
                    for gi in range(3):
                        p = k.psum()
                        for c in range(KC):
                            k.op('pe', lambda e, c=c, gi=gi: e.matmul(p[:, 0:TB], lhsT=ws[gi][:, c * 128:(c + 1) * 128], rhs=hblk[:, c, :], start=(c == 0), stop=(c == KC - 1)), reads=[ws[gi], hblk], writes=[p], inc=(c == KC - 1))
                        pgs.append(p)
                    pys = []
                    for bi, (bw, nk, fidx) in enumerate(((ba, 16, lambda kc: (kc // 4) * 8 + kc % 4), (bg, 8, lambda kc: (kc // 2) * 8 + 4 + kc % 2), (bh, 8, lambda kc: (kc // 2) * 8 + 6 + kc % 2))):
                        p = k.psum()
                        for kc in range(nk):
                            k.op('pe', lambda e, kc=kc: e.matmul(p[:, 0:TB], lhsT=bw[:, kc * 128:(kc + 1) * 128], rhs=ob[:, fidx(kc), :], start=(kc == 0), stop=(kc == nk - 1)), reads=[bw, ob], writes=[p], inc=(kc == nk - 1))
                        pys.append(p)
                    ms = []
                    for gi in range(3):
                        sg = sgp.next()
                        k.op('act', lambda e, gi=gi: e.activation(out=sg[:], in_=pgs[gi][:, 0:TB], func=AF.Sigmoid), reads=[pgs[gi]], writes=[sg])
                        m_ = mp.next()
                        k.op('dve', lambda e, gi=gi: e.tensor_tensor(out=m_[:], in0=sg[:], in1=pys[gi][:, 0:TB], op=ALU.mult), reads=[sg, pys[gi]], writes=[m_])
                        ms.append(m_)
                    k.op('dve', lambda e: e.tensor_tensor(out=ms[0][:], in0=ms[0][:], in1=ms[1][:], op=ALU.add), reads=[ms[0], ms[1]], writes=[ms[0]])
                    k.op('dve', lambda e, i=i: e.tensor_tensor(out=mg[:, i, :], in0=ms[0][:], in1=ms[2][:], op=ALU.add), reads=[ms[0], ms[2]], writes=[mg.part(i)])
                for i in range(KC):
                    w = wpool.next()
                    wtile('wo', l, i, w)
                    po = k.psum()
                    for c in range(KC):
                        k.op('pe', lambda e, c=c: e.matmul(po[:, 0:TB], lhsT=w[:, c * 128:(c + 1) * 128], rhs=mg[:, c, :], start=(c == 0), stop=(c == KC - 1)), reads=[w, mg.part(c)], writes=[po], inc=(c == KC - 1))
                    xc = xpool.next()
                    k.dma('sp', xc[:], xT[i * 128:(i + 1) * 128, t0:t0 + TB], reads=[xT.part(i)], writes=[xc])
                    k.op('dve', lambda e: e.tensor_tensor(out=xc[:], in0=po[:, 0:TB], in1=xc[:], op=ALU.add), reads=[po, xc], writes=[xc])
                    k.dma('pool', xT[i * 128:(i + 1) * 128, t0:t0 + TB], xc[:], reads=[xc], writes=[xT.part(i)])

    seq = []
    for l in range(L):
        seq += [('ffn1', lambda l=l: ffn(l, 1, xT, xT)), ('mixnorm', lambda l=l: mixnorm(l)), ('proj', lambda l=l: projection(l)),
                ('attn', lambda l=l: attention(l)), ('scan', lambda l=l: scans(l)), ('merge', lambda l=l: merge(l)),
                ('ffn2', lambda l=l: ffn(l, 2, xT, yT if (l == L - 1 and STOP is None) else xT))]
    done = False
    if STOP == 'prep':
        done = True
    for nm, fn in seq:
        if done:
            break
        if STOP is not None and STOP.startswith('-') and nm == STOP[1:]:
            break
        fn()
        if nm == STOP:
            done = True
    if STOP is not None:
        k.dma('pool', yT[:, :], xT[:, :], reads=[xT], writes=[yT])
    k.barrier()
    pp.__exit__(None, None, None)
    return nc


PROJ_OFF = dict(a_q=0, a_k=2048, a_v=4096, g_q=6144, g_k=6656, g_v=7168, g_r=8192, lr_f=9216, lr_b=9232,
                h_q=9248, h_zf=10272, h_zb=11296, h_i=12320, h_g=13344, gate=14368)


def kernel(**inp):
    inp = {kk: np.asarray(v) for kk, v in inp.items()}
    x = inp['x']
    B, S, D = x.shape
    L = inp['ffn1_norm'].shape[0]
    F = inp['ffn1_w_out'].shape[1]
    cfg = Cfg(D=D, S=S, L=L, F=F)
    T, KC = cfg.T, cfg.KC
    voff, NV = vec_layout(cfg)
    nc = build(cfg)
    f32 = np.float32
    w_in = inp['w_in']

    packed = {}
    for l in range(L):
        packed['f1i', l] = pack_fm(inp['ffn1_w_in'][l])
        packed['f1o', l] = pack_fm(inp['ffn1_w_out'][l])
        packed['gat', l] = pack_fm(w_in[l][:, PROJ_OFF['gate']:PROJ_OFF['gate'] + 3 * D])
        packed['bra', l] = pack_fm(inp['w_branch_attn'][l])
        packed['brg', l] = pack_fm(inp['w_branch_gla'][l])
        packed['brh', l] = pack_fm(inp['w_branch_hgrn'][l])
        packed['wo', l] = pack_fm(inp['w_out'][l])
        packed['f2i', l] = pack_fm(inp['ffn2_w_in'][l])
        packed['f2o', l] = pack_fm(inp['ffn2_w_out'][l])
    kinds = ['f1i', 'f1o', 'gat', 'bra', 'brg', 'brh', 'wo', 'f2i', 'f2o']
    cpack = const_pack(cfg)
    cossin = cossin_tables(cfg)
    in_maps = []
    for c in range(8):
        b, r = c // 4, c % 4
        m = {}
        m['xT'] = np.ascontiguousarray(x[b, r * T:(r + 1) * T, :].T.astype(f32))
        for kn in kinds:
            parts = []
            for l in range(L):
                P_ = packed[kn, l]
                NT_ = P_.shape[0] // 128
                parts.append(P_.reshape(NT_, 4, 32, P_.shape[1])[:, r].reshape(NT_ * 32, P_.shape[1]))
            m['wsh_' + kn] = np.ascontiguousarray(np.concatenate(parts, 0))
        fm_tiles = []
        tm_groups = []
        for l in range(L):
            W = w_in[l]
            cols = []
            for hh in range(2):
                for mm in range(2):
                    cols.append((PROJ_OFF['a_q'] + (2 * r + hh) * 256 + mm * 128, 128))
            for hh in range(2):
                for mm in range(2):
                    cols.append((PROJ_OFF['a_k'] + (2 * r + hh) * 256 + mm * 128, 128))
            cols.append((PROJ_OFF['g_q'] + r * 128, 128))
            cols.append((PROJ_OFF['g_k'] + r * 128, 128))
            cols.append((PROJ_OFF['g_r'] + r * 256, 128))
            cols.append((PROJ_OFF['g_r'] + r * 256 + 128, 128))
            cols.append((PROJ_OFF['lr_f'], 16))
            cols.append((PROJ_OFF['lr_b'], 16))
            for nm in ('h_q', 'h_zf', 'h_zb', 'h_g'):
                for hh in range(2):
                    cols.append((PROJ_OFF[nm] + (2 * r + hh) * 128, 128))
            assert len(cols) == cfg.NFM
            for (c0, w) in cols:
                fm_tiles.append(pack_fm(pad_cols(W[:, c0:c0 + w], 128)))
            g0 = W[:, PROJ_OFF['a_v'] + 2 * r * 256: PROJ_OFF['a_v'] + (2 * r + 2) * 256]
            g1 = np.concatenate([W[:, PROJ_OFF['g_v'] + r * 256: PROJ_OFF['g_v'] + (r + 1) * 256],
                                 W[:, PROJ_OFF['h_i'] + 2 * r * 128: PROJ_OFF['h_i'] + (2 * r + 2) * 128]], 1)
            tm_groups.append(pack_tm(g0))
            tm_groups.append(pack_tm(g1))
        m['hfm'] = np.ascontiguousarray(np.concatenate(fm_tiles, 0))
        m['htm'] = np.ascontiguousarray(np.concatenate(tm_groups, 0))
        v = np.zeros((128, NV), f32)

        def put(nm, idx, col):
            v[:, voff[nm] + idx] = col
        for l in range(L):
            for cc in range(KC):
                put('n1', l * KC + cc, inp['ffn1_norm'][l, cc * 128:(cc + 1) * 128])
                put('nm', l * KC + cc, inp['mix_norm'][l, cc * 128:(cc + 1) * 128])
                put('n2', l * KC + cc, inp['ffn2_norm'][l, cc * 128:(cc + 1) * 128])
            put('qn', l, inp['attn_q_norm'][l])
            put('kn', l, inp['attn_k_norm'][l])
            for j in range(4):
                put('lam', l * 4 + j, inp['attn_lambda'][l, j])
            for et in range(2):
                put('sub', l * 2 + et, inp['attn_sub_norm'][l, et * 128:(et + 1) * 128])
                put('gon', l * 2 + et, inp['gla_out_norm'][l, et * 128:(et + 1) * 128])
            put('hon', l, inp['hgrn_out_norm'][l])
            for d, nm in enumerate(('fwd', 'bwd')):
                put('gbias', l * 2 + d, inp['gla_gate_b_' + nm][l, r * 128:(r + 1) * 128])
                v[0:16, voff['w2'] + (l * 2 + d) * 128: voff['w2'] + (l * 2 + d + 1) * 128] = inp['gla_gate_w2_' + nm][l][:, r * 128:(r + 1) * 128]
                for hh in range(2):
                    put('lb', (d * 2 + hh) * L + l, inp['hgrn_lb_' + nm][l, (2 * r + hh) * 128:(2 * r + hh + 1) * 128])
        v[:, voff['sel'] + r] = 1.0
        m['vecs'] = v
        m['cpack'] = cpack
        m['cossin'] = cossin
        in_maps.append(m)
    res = run_bass_kernel_spmd(nc, in_maps, core_ids=list(range(8)))
    out = np.zeros((B, S, D), f32)
    for c in range(8):
        b, r = c // 4, c % 4
        out[b, r * T:(r + 1) * T, :] = res.results[c]['yT'].T
    return out
```

```python
import contextlib
import math
import numpy as np
import concourse.bass as bass
import concourse.mybir as mybir
from concourse.bass_utils import run_bass_kernel_spmd

F32 = mybir.dt.float32
BF16 = mybir.dt.bfloat16
AF = mybir.ActivationFunctionType
ALU = mybir.AluOpType
EPS = 1e-6
KD = 8
STOP = None
ENG = ['pe', 'act', 'dve', 'pool', 'sp']


class Cfg:
    def __init__(s, D=4096, S=4096, L=4, F=3072):
        s.D, s.S, s.L, s.F = D, S, L, F
        s.T = S // 4
        s.TB = min(512, s.T)
        s.KC = D // 128
        s.FC = F // 128
        s.NTB = s.T // s.TB
        s.NSB = S // s.TB
        s.QG = min(512, S)
        s.NFM = 22
        s.CHB = (1 << 20) if D >= 2048 else (1 << 16)


class Buf:
    def __init__(self, t, name):
        self.t, self.name = t, name
        self.w = None
        self.r = {}
        self.parts = {}
        self.parent = None

    def part(self, key):
        if key not in self.parts:
            b = Buf(self.t, "%s.%s" % (self.name, key))
            b.parent = self
            self.parts[key] = b
        return self.parts[key]

    def __getitem__(self, idx):
        return self.t[idx]

    def wev(self):
        ev = [self.w] if self.w else []
        if self.parent is not None and self.parent.w:
            ev.append(self.parent.w)
        for p in self.parts.values():
            if p.w:
                ev.append(p.w)
        return ev

    def rev(self):
        ev = list(self.r.items())
        if self.parent is not None:
            ev += list(self.parent.r.items())
        for p in self.parts.values():
            ev += list(p.r.items())
        return ev


class K:
    def __init__(self, nc):
        self.nc = nc
        self.es = contextlib.ExitStack()
        self.e = {'pe': nc.tensor, 'act': nc.scalar, 'dve': nc.vector, 'pool': nc.gpsimd, 'sp': nc.sync}
        self.semobj = {}
        for k in ENG:
            self.semobj[k] = self.es.enter_context(nc.semaphore("s_" + k))
        self.cnt = {k: 0 for k in ENG}
        self.known = {k: {} for k in ENG}
        self.dq = {}
        for q in ('sp', 'pool'):
            for i in range(KD):
                self.semobj[('d', q, i)] = self.es.enter_context(nc.semaphore("d_%s%d" % (q, i)))
            self.dq[q] = 0
        self.ncc = 0
        self.NCS = 4
        for i in range(self.NCS):
            self.semobj[('cc', i)] = self.es.enter_context(nc.semaphore("cc%d" % i))
        self.nps = 0
        self.ps = []
        for i in range(8):
            t = self.es.enter_context(nc.psum_tensor("psb%d" % i, [128, 512], F32))
            self.ps.append(Buf(t, "ps%d" % i))
        self.uid = 0

    def psum(self):
        b = self.ps[self.nps % 8]
        self.nps += 1
        return b

    def _wait(self, eng, key, val):
        if self.known[eng].get(key, 0) >= val:
            return
        self.e[eng].wait_ge(self.semobj[key], val)
        self.known[eng][key] = val

    def _deps(self, eng, reads, writes):
        mx = {}
        for b in reads:
            for (k, v) in b.wev():
                mx[k] = max(mx.get(k, 0), v)
        for b in writes:
            for (k, v) in b.wev() + b.rev():
                if k == eng and eng == 'pe':
                    continue
                mx[k] = max(mx.get(k, 0), v)
        for k, v in mx.items():
            self._wait(eng, k, v)

    def _mark(self, ev, reads, writes):
        k, v = ev
        for b in reads:
            b.r[k] = max(b.r.get(k, 0), v)
        for b in writes:
            b.w = ev
            b.r = {}
            for p in b.parts.values():
                p.w = None
                p.r = {}

    def op(self, eng, fn, reads=(), writes=(), inc=True):
        self._deps(eng, reads, writes)
        ins = fn(self.e[eng])
        inc = True
        if inc:
            ins.then_inc(self.semobj[eng], 1)
            self.cnt[eng] += 1
            ev = (eng, self.cnt[eng])
        else:
            ev = (eng, self.cnt[eng] + 1)
        self._mark(ev, reads, writes)

    def dma(self, q, out_ap, in_ap, reads=(), writes=()):
        i = self.dq[q]
        self.dq[q] += 1
        key = ('d', q, i % KD)
        val = 16 * (i // KD + 1)
        if i >= KD:
            self._wait(q, key, val - 16)
        self._deps(q, reads, writes)
        self.e[q].dma_start(out=out_ap, in_=in_ap).then_inc(self.semobj[key], 16)
        self._mark((key, val), reads, writes)

    def allgather(self, in_t, out_t, groups, reads, writes):
        i = self.ncc
        self.ncc += 1
        key = ('cc', i % self.NCS)
        val = i // self.NCS + 1
        self._deps('pool', reads, writes)
        self.e['pool'].collective_compute(
            "AllGather", ALU.bypass, replica_groups=groups,
            ins=[in_t.ap().opt()], outs=[out_t.ap().opt()]).then_inc(self.semobj[key])
        self._mark((key, val), reads, writes)

    def barrier(self):
        evs = [(k, self.cnt[k]) for k in ENG if self.cnt[k] > 0]
        for q in ('sp', 'pool'):
            n = self.dq[q]
            for s in range(min(KD, n)):
                last = ((n - 1 - s) // KD) * KD + s
                evs.append((('d', q, s), 16 * (last // KD + 1)))
        for s in range(min(self.NCS, self.ncc)):
            evs.append((('cc', s), (self.ncc - 1 - s) // self.NCS + 1))
        for e in ENG:
            for (k, v) in evs:
                if k != e:
                    self._wait(e, k, v)

    def dram(self, name, shape, dt, **kw):
        t = self.nc.dram_tensor(name, list(shape), dt, **kw)
        return Buf(t, name)


class Phase:
    def __init__(self, k):
        self.k = k
        self.es = contextlib.ExitStack()

    def __enter__(self):
        return self

    def __exit__(self, *a):
        self.k.barrier()
        self.es.close()
        return False

    def sb(self, name, shape, dt):
        self.k.uid += 1
        nm = "%s_%d" % (name, self.k.uid)
        t = self.es.enter_context(self.k.nc.sbuf_tensor(nm, list(shape), dt))
        return Buf(t, nm)

    def pool(self, name, shape, dt, n):
        return Rot([self.sb(name + str(i), shape, dt) for i in range(n)])


class Rot:
    def __init__(self, bufs):
        self.bufs = bufs
        self.i = 0

    def next(self):
        b = self.bufs[self.i % len(self.bufs)]
        self.i += 1
        return b


def const_pack(cfg):
    j = np.arange(128)[:, None]
    i = np.arange(128)[None, :]
    same = (j // 64) == (i // 64)
    ones = np.ones((128, 128), np.float32)
    ident = np.eye(128, dtype=np.float32)
    RT = np.zeros((128, 128), np.float32)
    for m in range(128):
        if m < 64:
            RT[m + 64, m] = -1.0
        else:
            RT[m - 64, m] = 1.0
    tri_f = (same & (j <= i)).astype(np.float32)
    tri_b = (same & (j >= i)).astype(np.float32)
    mid_f = (same & ((j % 64) <= 31)).astype(np.float32)
    mid_b = (same & ((j % 64) >= 32)).astype(np.float32)
    a_f = tri_f - mid_f
    a_b = tri_b - mid_b
    kh_f = (same & (j > i)).astype(np.float32)
    kh_b = (same & (j < i)).astype(np.float32)
    totc = np.zeros((128, 2), np.float32)
    totc[:64, 0] = 1.0
    totc[64:, 1] = 1.0
    pack = np.concatenate([ones, ident, RT, tri_f, tri_b, a_f, a_b, kh_f, kh_b, totc], axis=1)
    return np.ascontiguousarray(pack)


C_ONES, C_ID, C_RT, C_TRIF, C_TRIB, C_AF, C_AB, C_KHF, C_KHB, C_TOTC = [128 * n for n in range(10)]
C_N = 128 * 9 + 2


def cossin_tables(cfg):
    half = 64
    inv = (10000.0 ** (-np.arange(half, dtype=np.float32) / half)).astype(np.float32)
    pos = np.arange(cfg.S, dtype=np.float32)
    ang = (pos[:, None] * inv[None, :]).astype(np.float32)
    cos = np.cos(ang).astype(np.float32).T
    sin = np.sin(ang).astype(np.float32).T
    cs = np.stack([np.concatenate([cos, cos], 0), np.concatenate([sin, sin], 0)], 1)
    return np.ascontiguousarray(cs.astype(np.float32))


def pack_fm(W):
    Kd, N = W.shape
    kc, nt = Kd // 128, N // 128
    return np.ascontiguousarray(W.reshape(kc, 128, nt, 128).transpose(2, 1, 0, 3).reshape(nt * 128, Kd))


def pad_cols(W, n):
    if W.shape[1] == n:
        return W
    out = np.zeros((W.shape[0], n), W.dtype)
    out[:, :W.shape[1]] = W
    return out


def pack_tm(W):
    Kd, N = W.shape
    kc = Kd // 128
    return np.ascontiguousarray(W.reshape(kc, 128, N).transpose(1, 0, 2).reshape(128, kc * N))


def vec_layout(cfg):
    L, KC = cfg.L, cfg.KC
    off = {}
    n = 0
    for nm, sz in [('n1', L * KC), ('nm', L * KC), ('n2', L * KC), ('qn', L), ('kn', L), ('lam', L * 4),
                   ('sub', L * 2), ('gon', L * 2), ('hon', L), ('gbias', L * 2), ('lb', 2 * 2 * L),
                   ('w2', L * 2 * 128), ('sel', 4)]:
        off[nm] = n
        n += sz
    return off, n


def build(cfg):
    D, S, L, F, T, TB, KC, FC = cfg.D, cfg.S, cfg.L, cfg.F, cfg.T, cfg.TB, cfg.KC, cfg.FC
    nc = bass.Bass(target_bir_lowering=False)
    k = K(nc)
    voff, NV = vec_layout(cfg)
    lam_init = [0.8 - 0.6 * math.exp(-0.3 * l) for l in range(L)]

    def ext(name, shape, dt=F32):
        return Buf(nc.dram_tensor(name, list(shape), dt, kind="ExternalInput"), name)

    xT_in = ext("xT", [D, T])
    yT = Buf(nc.dram_tensor("yT", [D, T], F32, kind="ExternalOutput"), "yT")
    kinds = {'f1i': (2 * F, D), 'f1o': (D, F), 'gat': (3 * D, D), 'bra': (D, 2048), 'brg': (D, 1024),
             'brh': (D, 1024), 'wo': (D, D), 'f2i': (2 * F, D), 'f2o': (D, F)}
    wsh = {kn: ext("wsh_" + kn, [L * (N // 4), Kd]) for kn, (N, Kd) in kinds.items()}
    hfm_in = ext("hfm", [L * cfg.NFM * 128, D])
    htm_in = ext("htm", [L * 2 * 128, KC * 512])
    vecs_in = ext("vecs", [128, NV])
    cpack_in = ext("cpack", [128, C_N])
    cs_in = ext("cossin", [128, 2, S])

    xT = k.dram("xTw", [D, T], F32)
    def mtiles(NT, Kd):
        m = max(1, min(NT, cfg.CHB // (32 * Kd * 2)))
        while NT % m:
            m -= 1
        return m
    wmt = {kn: mtiles(N // 128, Kd) for kn, (N, Kd) in kinds.items()}
    wbs = {}
    wfull = {}
    for kn, (N, Kd) in kinds.items():
        m = wmt[kn]
        for l in range(L):
            for j in range(N // 128 // m):
                wbs[kn, l, j] = k.dram("wb_%s%d_%d" % (kn, l, j), [m * 32, Kd], BF16)
                wfull[kn, l, j] = k.dram("wf_%s%d_%d" % (kn, l, j), [4 * m * 32, Kd], BF16)

    def wtile(kn, l, nt, dst):
        m = wmt[kn]
        j, t = nt // m, nt % m
        src_ = wfull[kn, l, j]
        for r2 in range(4):
            k.dma('sp', dst[32 * r2:32 * r2 + 32, :], src_[r2 * m * 32 + t * 32: r2 * m * 32 + t * 32 + 32, :], reads=[src_], writes=[dst])
    hfm = k.dram("hfmb", [L * cfg.NFM * 128, D], BF16)
    htm = k.dram("htmb", [L * 2 * 128, KC * 512], BF16)
    HR = max(128, min(D, (cfg.CHB // (T * 2)) // 128 * 128))
    NHC = D // HR
    hT_own = [k.dram("hT_own%d" % j, [HR, T], BF16) for j in range(NHC)]
    hT_full = [k.dram("hT_full%d" % j, [4 * HR, T], BF16) for j in range(NHC)]
    OR = max(128, min(1024, (cfg.CHB // (T * 2)) // 128 * 128))
    NOP = 1024 // OR
    o_mine = {(s, q): k.dram("o_mine%d_%d" % (s, q), [OR, T], BF16) for s in range(4) for q in range(NOP)}
    o_full = {(s, q): k.dram("o_full%d_%d" % (s, q), [4 * OR, T], BF16) for s in range(4) for q in range(NOP)}

    def o_store(feat0, t0, width, src):
        off = 0
        q, f_ = feat0 // OR, feat0 % OR
        while off < width:
            t = t0 + off
            slab, tl = t // T, t % T
            w = min(width - off, T - tl)
            k.dma('pool', o_mine[slab, q][f_: f_ + 128, tl:tl + w], src[:, off:off + w], reads=[src], writes=[o_mine[slab, q]])
            off += w
    aq = k.dram("aq", [4, 128, S], BF16)
    ak = k.dram("ak", [4, 128, S], BF16)
    av = k.dram("av", [S, 512], BF16)
    gv = k.dram("gv", [S, 512], BF16)
    gqT = k.dram("gqT", [128, S], F32)
    gkT = k.dram("gkT", [128, S], F32)
    ggT = k.dram("ggT", [2, 128, S], F32)
    grT = k.dram("grT", [2, 128, S], F32)
    hqT = k.dram("hqT", [2, 128, S], F32)
    hkT = k.dram("hkT", [4, 128, S], F32)
    hgT = k.dram("hgT", [4, 128, S], F32)
    hoT = k.dram("hoT", [2, 128, S], F32)
    GROUPS4 = [[0, 1, 2, 3], [4, 5, 6, 7]]
    GROUPS8 = [list(range(8))]

    pp = Phase(k)
    cp = pp.sb("cp", [128, C_N], F32)
    cpb = pp.sb("cpb", [128, 3 * 128], BF16)
    vecs = pp.sb("vecs", [128, NV], F32)
    lbv = pp.sb("lbv", [128, 3, 4 * L], F32)
    lamv = pp.sb("lamv", [128, L, 2], F32)
    subg = pp.sb("subg", [128, L * 2], F32)
    w2b = pp.sb("w2b", [16, L * 2 * 128], BF16)
    negb = pp.sb("negb", [128, L * 2], F32)
    epsb = pp.sb("epsb", [128, 2], F32)
    k.op('dve', lambda e: e.memset(epsb[:, 0:1], EPS), writes=[epsb])
    k.op('dve', lambda e: e.memset(epsb[:, 1:2], 1.0), reads=[epsb], writes=[epsb])

    k.dma('sp', cp[:], cpack_in[:, :], writes=[cp])
    k.dma('sp', vecs[:], vecs_in[:, :], writes=[vecs])
    k.op('dve', lambda e: e.tensor_copy(out=cpb[:], in_=cp[:, 0:384]), reads=[cp], writes=[cpb])
    ones_b = lambda: cpb[:, 0:128]
    RT_b = lambda: cpb[:, 256:384]
    ident_f = lambda: cp[:, C_ID:C_ID + 128]

    k.dma('pool', xT[:, :], xT_in[:, :], writes=[xT])
    k.dma('pool', hfm[:, :], hfm_in[:, :], writes=[hfm])
    k.dma('pool', htm[:, :], htm_in[:, :], writes=[htm])
    order = ['f1i', 'f1o', 'gat', 'bra', 'brg', 'brh', 'wo', 'f2i', 'f2o']

    def weight_prep(l):
        for kn in order:
            N, Kd = kinds[kn]
            rows = N // 4
            m = wmt[kn]
            for j in range(N // 128 // m):
                k.dma('pool', wbs[kn, l, j][:, :], wsh[kn][l * rows + j * m * 32: l * rows + (j + 1) * m * 32, :], writes=[wbs[kn, l, j]])
                k.allgather(wbs[kn, l, j].t, wfull[kn, l, j].t, GROUPS4, reads=[wbs[kn, l, j]], writes=[wfull[kn, l, j]])
    weight_prep(0)

    V = lambda nm, i, n=1: vecs[:, voff[nm] + i: voff[nm] + i + n]
    with Phase(k) as ph:
        e_ = ph.sb("lbe", [128, 4, L], F32)
        s_ = ph.sb("lbs", [128, 4], F32)
        p_ = ph.sb("lbp", [128, 4, L], F32)
        lb_ = ph.sb("lbl", [128, 4, L], F32)
        raw = vecs[:, voff['lb']: voff['lb'] + 4 * L].rearrange("p (a l) -> p a l", l=L)
        k.op('act', lambda e: e.activation(out=e_[:], in_=raw, func=AF.Exp), reads=[vecs], writes=[e_])
        k.op('dve', lambda e: e.tensor_copy(out=s_[:], in_=e_[:, :, 0]), reads=[e_], writes=[s_])
        for l in range(1, L):
            k.op('dve', lambda e, l=l: e.tensor_tensor(out=s_[:], in0=s_[:], in1=e_[:, :, l], op=ALU.add), reads=[e_, s_], writes=[s_])
        k.op('dve', lambda e: e.reciprocal(out=s_[:], in_=s_[:]), reads=[s_], writes=[s_])
        for l in range(L):
            k.op('dve', lambda e, l=l: e.tensor_tensor(out=p_[:, :, l], in0=e_[:, :, l], in1=s_[:], op=ALU.mult), reads=[e_, s_, p_], writes=[p_])
        k.op('dve', lambda e: e.memset(lb_[:, :, 0], 0.0), reads=[lb_], writes=[lb_])
        for l in range(1, L):
            k.op('dve', lambda e, l=l: e.tensor_tensor(out=lb_[:, :, l], in0=lb_[:, :, l - 1], in1=p_[:, :, l], op=ALU.add), reads=[lb_, p_], writes=[lb_])
        lbf = lb_[:].rearrange("p a l -> p (a l)")
        k.op('dve', lambda e: e.tensor_scalar(out=lbf, in0=lbf, scalar1=0.0, scalar2=1.0 - 1e-6, op0=ALU.max, op1=ALU.min), reads=[lb_], writes=[lb_])
        k.op('dve', lambda e: e.tensor_scalar(out=lbv[:, 0, :], in0=lbf, scalar1=-1.0, scalar2=1.0, op0=ALU.mult, op1=ALU.add), reads=[lb_, lbv], writes=[lbv])
        k.op('dve', lambda e: e.tensor_scalar(out=lbv[:, 1, :], in0=lbf, scalar1=1e-20, scalar2=None, op0=ALU.max), reads=[lb_, lbv], writes=[lbv])
        k.op('dve', lambda e: e.tensor_scalar(out=lbv[:, 2, :], in0=lbv[:, 0, :], scalar1=-1.0, scalar2=None, op0=ALU.mult), reads=[lbv], writes=[lbv])
        pr = ph.sb("lpr", [128, L * 2], BF16)
        le = ph.sb("lle", [128, L * 2], F32)
        for l in range(L):
            for m in range(2):
                k.op('dve', lambda e, l=l, m=m: e.tensor_tensor(out=pr[:, l * 2 + m: l * 2 + m + 1], in0=V('lam', l * 4 + 2 * m), in1=V('lam', l * 4 + 2 * m + 1), op=ALU.mult), reads=[vecs, pr], writes=[pr])
        pz = k.psum()
        k.op('pe', lambda e: e.matmul(pz[:, 0:L * 2], lhsT=ones_b(), rhs=pr[:], start=True, stop=True), reads=[cpb, pr], writes=[pz])
        k.op('act', lambda e: e.activation(out=le[:], in_=pz[:, 0:L * 2], func=AF.Exp), reads=[pz], writes=[le])
        for l in range(L):
            k.op('dve', lambda e, l=l: e.scalar_tensor_tensor(out=lamv[:, l, 0:1], in0=le[:, 2 * l + 1: 2 * l + 2], scalar=-lam_init[l], in1=le[:, 2 * l: 2 * l + 1], op0=ALU.add, op1=ALU.subtract), reads=[le, lamv], writes=[lamv])
            k.op('dve', lambda e, l=l: e.tensor_scalar(out=subg[:, 2 * l: 2 * l + 2], in0=V('sub', 2 * l, 2), scalar1=1.0 - lam_init[l], scalar2=None, op0=ALU.mult), reads=[vecs, subg], writes=[subg])
        k.op('dve', lambda e: e.tensor_copy(out=w2b[:], in_=vecs[0:16, voff['w2']: voff['w2'] + L * 2 * 128]), reads=[vecs], writes=[w2b])
        k.op('dve', lambda e: e.tensor_scalar(out=negb[:], in0=V('gbias', 0, 2 * L), scalar1=-1.0, scalar2=None, op0=ALU.mult), reads=[vecs], writes=[negb])

    def rstd_from_ps(ph, ps_ap, ps_buf, n, width, rpool):
        r = rpool.next()
        k.op('act', lambda e: e.activation(out=r[:, 0:width], in_=ps_ap, func=AF.Ln, bias=epsb[:, 0:1], scale=1.0 / n), reads=[ps_buf, epsb], writes=[r])
        k.op('act', lambda e: e.activation(out=r[:, 0:width], in_=r[:, 0:width], func=AF.Exp, scale=-0.5), reads=[r], writes=[r])
        return r

    def norm_block(ph, xsrc, t0, gname, l, hblk, xpool, sqpool, rpool):
        pss = k.psum()
        for c in range(KC):
            xc = xpool.next()
            k.dma('sp', xc[:], xsrc[c * 128:(c + 1) * 128, t0:t0 + TB], reads=[xsrc.part(c)], writes=[xc])
            sq = sqpool.next()
            k.op('act', lambda e: e.activation(out=sq[:], in_=xc[:], func=AF.Square), reads=[xc], writes=[sq])
            k.op('pe', lambda e, c=c: e.matmul(pss[:, 0:TB], lhsT=ones_b(), rhs=sq[:], start=(c == 0), stop=(c == KC - 1)), reads=[cpb, sq], writes=[pss], inc=(c == KC - 1))
        r = rstd_from_ps(ph, pss[:, 0:TB], pss, D, TB, rpool)
        for c in range(KC):
            xc = xpool.next()
            k.dma('sp', xc[:], xsrc[c * 128:(c + 1) * 128, t0:t0 + TB], reads=[xsrc.part(c)], writes=[xc])
            k.op('dve', lambda e, c=c: e.scalar_tensor_tensor(out=hblk[:, c, :], in0=xc[:], scalar=V(gname, l * KC + c), in1=r[:, 0:TB], op0=ALU.mult, op1=ALU.mult), reads=[xc, r, vecs], writes=[hblk.part(c)])

    def ffn(l, which, xsrc, xdst):
        kin, kout = 'f%di' % which, 'f%do' % which
        gname = 'n%d' % which
        with Phase(k) as ph:
            hblk = ph.sb("hblk", [128, KC, TB], BF16)
            hid = ph.sb("hid", [128, FC, TB], BF16)
            xpool = ph.pool("xc", [128, TB], F32, 3)
            sqpool = ph.pool("sq", [128, TB], BF16, 2)
            rpool = ph.pool("rs", [128, TB], F32, 1)
            wpool = ph.pool("wt", [128, D], BF16, 4)
            wopool = ph.pool("wo", [128, F], BF16, 2)
            sgpool = ph.pool("sg", [128, TB], F32, 2)
            for tb in range(cfg.NTB):
                t0 = tb * TB
                norm_block(ph, xsrc, t0, gname, l, hblk, xpool, sqpool, rpool)
                for j in range(FC):
                    wg, wu = wpool.next(), wpool.next()
                    wtile(kin, l, j, wg)
                    wtile(kin, l, FC + j, wu)
                    pg, pu = k.psum(), k.psum()
                    for c in range(KC):
                        k.op('pe', lambda e, c=c: e.matmul(pg[:, 0:TB], lhsT=wg[:, c * 128:(c + 1) * 128], rhs=hblk[:, c, :], start=(c == 0), stop=(c == KC - 1)), reads=[wg, hblk.part(c)], writes=[pg], inc=(c == KC - 1))
                    for c in range(KC):
                        k.op('pe', lambda e, c=c: e.matmul(pu[:, 0:TB], lhsT=wu[:, c * 128:(c + 1) * 128], rhs=hblk[:, c, :], start=(c == 0), stop=(c == KC - 1)), reads=[wu, hblk.part(c)], writes=[pu], inc=(c == KC - 1))
                    sg = sgpool.next()
                    k.op('act', lambda e: e.activation(out=sg[:], in_=pg[:, 0:TB], func=AF.Silu), reads=[pg], writes=[sg])
                    k.op('dve', lambda e, j=j: e.tensor_tensor(out=hid[:, j, :], in0=sg[:], in1=pu[:, 0:TB], op=ALU.mult), reads=[sg, pu], writes=[hid.part(j)])
                for i in range(KC):
                    wo_ = wopool.next()
                    wtile(kout, l, i, wo_)
                    po = k.psum()
                    for fc in range(FC):
                        k.op('pe', lambda e, fc=fc: e.matmul(po[:, 0:TB], lhsT=wo_[:, fc * 128:(fc + 1) * 128], rhs=hid[:, fc, :], start=(fc == 0), stop=(fc == FC - 1)), reads=[wo_, hid.part(fc)], writes=[po], inc=(fc == FC - 1))
                    xc = xpool.next()
                    k.dma('sp', xc[:], xsrc[i * 128:(i + 1) * 128, t0:t0 + TB], reads=[xsrc.part(i)], writes=[xc])
                    k.op('dve', lambda e: e.scalar_tensor_tensor(out=xc[:], in0=po[:, 0:TB], scalar=0.5, in1=xc[:], op0=ALU.mult, op1=ALU.add), reads=[po, xc], writes=[xc])
                    k.dma('pool', xdst[i * 128:(i + 1) * 128, t0:t0 + TB], xc[:], reads=[xc], writes=[xdst.part(i)])

    def mixnorm(l):
        with Phase(k) as ph:
            hblk = ph.sb("hblk", [128, KC, TB], BF16)
            xpool = ph.pool("xc", [128, TB], F32, 3)
            sqpool = ph.pool("sq", [128, TB], BF16, 2)
            rpool = ph.pool("rs", [128, TB], F32, 1)
            for tb in range(cfg.NTB):
                t0 = tb * TB
                norm_block(ph, xT, t0, 'nm', l, hblk, xpool, sqpool, rpool)
                for j in range(NHC):
                    k.dma('pool', hT_own[j][:, t0:t0 + TB].rearrange("(c p) t -> p c t", p=128), hblk[:, j * (HR // 128):(j + 1) * (HR // 128), :], reads=[hblk], writes=[hT_own[j]])
        for j in range(NHC):
            k.allgather(hT_own[j].t, hT_full[j].t, GROUPS4, reads=[hT_own[j]], writes=[hT_full[j]])

    def projection(l):
        with Phase(k) as ph:
            hblk = ph.sb("hblk", [128, KC, TB], BF16)
            wpool = ph.pool("wt", [128, D], BF16, 3)
            tmw = ph.pool("tmw", [128, KC, 512], BF16, 2)
            cs = ph.sb("cs", [128, 2, TB], F32)
            sqpool = ph.pool("sq", [128, TB], BF16, 2)
            rpool = ph.pool("rs", [128, TB], F32, 2)
            st32 = ph.pool("st32", [128, TB], F32, 4)
            stb = ph.pool("stb", [128, TB], BF16, 4)
            tmp = ph.pool("tmp", [128, TB], F32, 4)
            vst = ph.pool("vst", [128, 512], BF16, 3)
            lr16 = ph.pool("lr16", [16, TB], BF16, 2)
            base = l * cfg.NFM * 128
            for sb_ in range(cfg.NSB):
                t0 = sb_ * TB
                slab = t0 // T
                tl = t0 - slab * T
                for j in range(NHC):
                    k.dma('sp', hblk[:, j * (HR // 128):(j + 1) * (HR // 128), :], hT_full[j][slab * HR:(slab + 1) * HR, tl:tl + TB].rearrange("(c p) t -> p c t", p=128), reads=[hT_full[j]], writes=[hblk])
                k.dma('sp', cs[:], cs_in[:, :, t0:t0 + TB], writes=[cs])

                def fm_tile(ti):
                    w = wpool.next()
                    k.dma('sp', w[:], hfm[base + ti * 128: base + (ti + 1) * 128, :], reads=[hfm], writes=[w])
                    p = k.psum()
                    for c in range(KC):
                        k.op('pe', lambda e, c=c: e.matmul(p[:, 0:TB], lhsT=w[:, c * 128:(c + 1) * 128], rhs=hblk[:, c, :], start=(c == 0), stop=(c == KC - 1)), reads=[w, hblk], writes=[p], inc=(c == KC - 1))
                    return p

                def store32(p_ap, pbuf, dst_ap, dstbuf, scale=None):
                    s = st32.next()
                    if scale is None:
                        k.op('act', lambda e: e.activation(out=s[:], in_=p_ap, func=AF.Copy), reads=[pbuf], writes=[s])
                    else:
                        k.op('act', lambda e: e.activation(out=s[:], in_=p_ap, func=AF.Copy, scale=scale), reads=[pbuf], writes=[s])
                    k.dma('pool', dst_ap, s[:], reads=[s], writes=[dstbuf])

                for ti in range(8):
                    p = fm_tile(ti)
                    gn = 'qn' if ti < 4 else 'kn'
                    sq = sqpool.next()
                    k.op('act', lambda e: e.activation(out=sq[:], in_=p[:, 0:TB], func=AF.Square), reads=[p], writes=[sq])
                    p2 = k.psum()
                    k.op('pe', lambda e: e.matmul(p2[:, 0:TB], lhsT=ones_b(), rhs=sq[:], start=True, stop=True), reads=[cpb, sq], writes=[p2])
                    r = rstd_from_ps(ph, p2[:, 0:TB], p2, 128, TB, rpool)
                    qn = tmp.next()
                    k.op('dve', lambda e: e.scalar_tensor_tensor(out=qn[:], in0=p[:, 0:TB], scalar=V(gn, l), in1=r[:, 0:TB], op0=ALU.mult, op1=ALU.mult), reads=[p, r, vecs], writes=[qn])
                    qb = stb.next()
                    k.op('act', lambda e: e.activation(out=qb[:], in_=qn[:], func=AF.Copy), reads=[qn], writes=[qb])
                    p3 = k.psum()
                    k.op('pe', lambda e: e.matmul(p3[:, 0:TB], lhsT=RT_b(), rhs=qb[:], start=True, stop=True), reads=[cpb, qb], writes=[p3])
                    t1 = tmp.next()
                    k.op('dve', lambda e: e.tensor_tensor(out=t1[:], in0=qn[:], in1=cs[:, 0, :], op=ALU.mult), reads=[qn, cs], writes=[t1])
                    t2 = tmp.next()
                    k.op('dve', lambda e: e.tensor_tensor(out=t2[:], in0=p3[:, 0:TB], in1=cs[:, 1, :], op=ALU.mult), reads=[p3, cs], writes=[t2])
                    ob = stb.next()
                    k.op('dve', lambda e: e.tensor_tensor(out=ob[:], in0=t1[:], in1=t2[:], op=ALU.add), reads=[t1, t2], writes=[ob])
                    dst = aq if ti < 4 else ak
                    k.dma('pool', dst[ti % 4, :, t0:t0 + TB], ob[:], reads=[ob], writes=[dst])
                p = fm_tile(8)
                store32(p[:, 0:TB], p, gqT[:, t0:t0 + TB], gqT, scale=128 ** -0.5)
                p = fm_tile(9)
                store32(p[:, 0:TB], p, gkT[:, t0:t0 + TB], gkT)
                for j in range(2):
                    p = fm_tile(10 + j)
                    store32(p[:, 0:TB], p, grT[j, :, t0:t0 + TB], grT)
                for d in range(2):
                    p = fm_tile(12 + d)
                    lr = lr16.next()
                    k.op('act', lambda e: e.activation(out=lr[:], in_=p[0:16, 0:TB], func=AF.Copy), reads=[p], writes=[lr])
                    p2 = k.psum()
                    k.op('pe', lambda e, d=d: e.matmul(p2[:, 0:TB], lhsT=w2b[:, (l * 2 + d) * 128:(l * 2 + d + 1) * 128], rhs=lr[:], start=True, stop=True), reads=[w2b, lr], writes=[p2])
                    e1 = tmp.next()
                    k.op('act', lambda e, d=d: e.activation(out=e1[:], in_=p2[:, 0:TB], func=AF.Exp, bias=negb[:, l * 2 + d: l * 2 + d + 1], scale=-1.0), reads=[p2, negb], writes=[e1])
                    k.op('act', lambda e: e.activation(out=e1[:], in_=e1[:], func=AF.Ln, bias=epsb[:, 1:2], scale=1.0), reads=[e1], writes=[e1])
                    s = st32.next()
                    k.op('dve', lambda e: e.tensor_scalar(out=s[:], in0=e1[:], scalar1=-1.0 / 16.0, scalar2=None, op0=ALU.mult), reads=[e1], writes=[s])
                    k.dma('pool', ggT[d, :, t0:t0 + TB], s[:], reads=[s], writes=[ggT])
                for h in range(2):
                    p = fm_tile(14 + h)
                    store32(p[:, 0:TB], p, hqT[h, :, t0:t0 + TB], hqT, scale=128 ** -0.5)
                for d in range(2):
                    for h in range(2):
                        p = fm_tile(16 + d * 2 + h)
                        li = (d * 2 + h) * L + l
                        e1 = tmp.next()
                        k.op('act', lambda e: e.activation(out=e1[:], in_=p[:, 0:TB], func=AF.Exp, scale=-1.0), reads=[p], writes=[e1])
                        k.op('dve', lambda e: e.tensor_scalar(out=e1[:], in0=e1[:], scalar1=1.0, scalar2=None, op0=ALU.add), reads=[e1], writes=[e1])
                        sg = tmp.next()
                        k.op('dve', lambda e: e.reciprocal(out=sg[:], in_=e1[:]), reads=[e1], writes=[sg])
                        f_ = tmp.next()
                        k.op('dve', lambda e, li=li: e.tensor_scalar(out=f_[:], in0=sg[:], scalar1=lbv[:, 0, li:li + 1], scalar2=lbv[:, 1, li:li + 1], op0=ALU.mult, op1=ALU.add), reads=[sg, lbv], writes=[f_])
                        s = st32.next()
                        k.op('act', lambda e: e.activation(out=s[:], in_=f_[:], func=AF.Ln), reads=[f_], writes=[s])
                        k.dma('pool', hgT[d * 2 + h, :, t0:t0 + TB], s[:], reads=[s], writes=[hgT])
                        s2 = st32.next()
                        k.op('dve', lambda e, li=li: e.tensor_scalar(out=s2[:], in0=sg[:], scalar1=lbv[:, 2, li:li + 1], scalar2=lbv[:, 0, li:li + 1], op0=ALU.mult, op1=ALU.add), reads=[sg, lbv], writes=[s2])
                        k.dma('pool', hkT[d * 2 + h, :, t0:t0 + TB], s2[:], reads=[s2], writes=[hkT])
                for h in range(2):
                    p = fm_tile(20 + h)
                    store32(p[:, 0:TB], p, hoT[h, :, t0:t0 + TB], hoT)
                for g in range(2):
                    w = tmw.next()
                    k.dma('sp', w[:], htm[(l * 2 + g) * 128:(l * 2 + g + 1) * 128, :].rearrange("p (c n) -> p c n", n=512), reads=[htm], writes=[w])
                    for sub in range(TB // 128):
                        p = k.psum()
                        for c in range(KC):
                            k.op('pe', lambda e, c=c, sub=sub: e.matmul(p[:, 0:512], lhsT=hblk[:, c, sub * 128:(sub + 1) * 128], rhs=w[:, c, :], start=(c == 0), stop=(c == KC - 1)), reads=[w, hblk], writes=[p], inc=(c == KC - 1))
                        vs = vst.next()
                        k.op('act', lambda e: e.activation(out=vs[:], in_=p[:, 0:512], func=AF.Copy), reads=[p], writes=[vs])
                        dst = av if g == 0 else gv
                        k.dma('pool', dst[t0 + sub * 128: t0 + (sub + 1) * 128, :], vs[:], reads=[vs], writes=[dst])

    def attention(l):
        QG = cfg.QG
        NK = S // 128
        with Phase(k) as ph:
            vt = ph.sb("vt", [128, NK, 256], BF16)
            ppool = ph.pool("pT", [128, QG], BF16, 4)
            o0n = ph.sb("o0n", [128, 2, QG], F32)
            od = ph.sb("od", [128, 2, QG], F32)
            rr = ph.pool("rr", [128, QG], F32, 2)
            tt_ = ph.pool("tt", [128, QG], F32, 2)
            sqb = ph.sb("sqb", [128, 2, QG], BF16)
            rpool = ph.pool("rs", [128, QG], F32, 1)
            ost = ph.pool("ost", [128, QG], BF16, 3)
            scale = 128 ** -0.5
            kq = {}
            for m in range(2):
                kq[m] = (ph.sb("kTm%d" % m, [128, S], BF16), ph.sb("qTm%d" % m, [128, S], BF16))
            accsets = Rot([[k.ps[0], k.ps[1], k.ps[2]], [k.ps[3], k.ps[4], k.ps[5]]])
            pspool = Rot([k.ps[6], k.ps[7]])
            for hh in range(2):
                k.dma('sp', vt[:], av[:, hh * 256:(hh + 1) * 256].rearrange("(c p) e -> p c e", p=128), reads=[av], writes=[vt])
                for m in range(2):
                    k.dma('sp', kq[m][0][:], ak[hh * 2 + m, :, :], reads=[ak], writes=[kq[m][0]])
                    k.dma('sp', kq[m][1][:], aq[hh * 2 + m, :, :], reads=[aq], writes=[kq[m][1]])
                for g in range(S // QG):
                    q0 = g * QG
                    for m in range(2):
                        kTm, qTm = kq[m]
                        acc0, acc1, accs = accsets.next()
                        for kc in range(NK):
                            pss = pspool.next()
                            k.op('pe', lambda e, kc=kc: e.matmul(pss[:, 0:QG], lhsT=kTm[:, kc * 128:(kc + 1) * 128], rhs=qTm[:, q0:q0 + QG], start=True, stop=True), reads=[kTm, qTm], writes=[pss])
                            pT = ppool.next()
                            k.op('act', lambda e: e.activation(out=pT[:], in_=pss[:, 0:QG], func=AF.Exp, scale=scale), reads=[pss], writes=[pT])
                            last = (kc == NK - 1)
                            k.op('pe', lambda e, kc=kc: e.matmul(acc0[:, 0:QG], lhsT=vt[:, kc, 0:128], rhs=pT[:], start=(kc == 0), stop=last), reads=[vt, pT], writes=[acc0], inc=False)
                            k.op('pe', lambda e, kc=kc: e.matmul(acc1[:, 0:QG], lhsT=vt[:, kc, 128:256], rhs=pT[:], start=(kc == 0), stop=last), reads=[vt, pT], writes=[acc1], inc=False)
                            k.op('pe', lambda e, kc=kc: e.matmul(accs[:, 0:QG], lhsT=ones_b(), rhs=pT[:], start=(kc == 0), stop=last), reads=[cpb, pT], writes=[accs], inc=True)
                        r = rr.next()
                        k.op('dve', lambda e: e.reciprocal(out=r[:], in_=accs[:, 0:QG]), reads=[accs], writes=[r])
                        if m == 0:
                            k.op('dve', lambda e: e.tensor_tensor(out=o0n[:, 0, :], in0=acc0[:, 0:QG], in1=r[:], op=ALU.mult), reads=[acc0, r, o0n], writes=[o0n])
                            k.op('dve', lambda e: e.tensor_tensor(out=o0n[:, 1, :], in0=acc1[:, 0:QG], in1=r[:], op=ALU.mult), reads=[acc1, r, o0n], writes=[o0n])
                        else:
                            for et, acc in ((0, acc0), (1, acc1)):
                                t_ = tt_.next()
                                k.op('dve', lambda e, acc=acc: e.tensor_tensor(out=t_[:], in0=acc[:, 0:QG], in1=r[:], op=ALU.mult), reads=[acc, r], writes=[t_])
                                k.op('dve', lambda e, et=et: e.scalar_tensor_tensor(out=od[:, et, :], in0=t_[:], scalar=lamv[:, l, 0:1], in1=o0n[:, et, :], op0=ALU.mult, op1=ALU.add), reads=[t_, lamv, o0n, od], writes=[od])
                    k.op('act', lambda e: e.activation(out=sqb[:], in_=od[:], func=AF.Square), reads=[od], writes=[sqb])
                    p2 = pspool.next()
                    for et in range(2):
                        k.op('pe', lambda e, et=et: e.matmul(p2[:, 0:QG], lhsT=ones_b(), rhs=sqb[:, et, :], start=(et == 0), stop=(et == 1)), reads=[cpb, sqb], writes=[p2], inc=(et == 1))
                    r = rstd_from_ps(ph, p2[:, 0:QG], p2, 256, QG, rpool)
                    for et in range(2):
                        ob = ost.next()
                        k.op('dve', lambda e, et=et: e.scalar_tensor_tensor(out=ob[:], in0=od[:, et, :], scalar=subg[:, 2 * l + et: 2 * l + et + 1], in1=r[:, 0:QG], op0=ALU.mult, op1=ALU.mult), reads=[od, subg, r], writes=[ob])
                        o_store(hh * 256 + et * 128, q0, QG, ob)

    def scans(l):
        NT = S // 128
        with Phase(k) as ph:
            chains = []
            osum = {}
            hd = [dict(name='g', q=lambda: gqT[:, :], dv=256, vcol=0, orow=512, gate=lambda j: grT[j], gain=('gon', 2 * l)),
                  dict(name='h0', q=lambda: hqT[0], dv=128, vcol=256, orow=768, gate=lambda j: hoT[0], gain=('hon', l)),
                  dict(name='h1', q=lambda: hqT[1], dv=128, vcol=384, orow=896, gate=lambda j: hoT[1], gain=('hon', l))]
            for hi_, h in enumerate(hd):
                osum[hi_] = ph.sb("osum" + h['name'], [128, h['dv'] // 128, S], F32)
                for d in range(2):
                    if hi_ == 0:
                        ksrc, gsrc = gkT[:, :], ggT[d]
                        kbuf, gbuf, qbuf = gkT, ggT, gqT
                    else:
                        ksrc, gsrc = hkT[d * 2 + hi_ - 1], hgT[d * 2 + hi_ - 1]
                        kbuf, gbuf, qbuf = hkT, hgT, hqT
                    Sf = ph.sb("S%d%d" % (hi_, d), [128, h['dv']], F32)
                    Sb = ph.sb("Sb%d%d" % (hi_, d), [128, h['dv']], BF16)
                    k.op('dve', lambda e, Sf=Sf: e.memset(Sf[:], 0.0), writes=[Sf])
                    k.op('dve', lambda e, Sb=Sb: e.memset(Sb[:], 0.0), writes=[Sb])
                    chains.append(dict(h=hi_, d=d, ksrc=ksrc, gsrc=gsrc, kbuf=kbuf, gbuf=gbuf, qbuf=qbuf, Sf=Sf, Sb=Sb))
            ld = ph.pool("ld", [128, 3, 128], F32, 3)
            vld = ph.pool("vld", [128, 256], BF16, 3)
            tk = ph.pool("tk", [128, 2, 128], F32, 3)
            ex = ph.pool("ex", [128, 4, 128], F32, 2)
            dec = ph.pool("dec", [128, 2], F32, 3)
            qk = ph.pool("qk", [128, 4, 128], BF16, 3)
            scm = ph.pool("scm", [128, 128], BF16, 3)
            written = {}
            for step in range(NT):
                for ch in chains:
                    h = hd[ch['h']]
                    d = ch['d']
                    dv = h['dv']
                    tt = step if d == 0 else NT - 1 - step
                    t0 = tt * 128
                    L_ = ld.next()
                    k.dma('sp', L_[:, 0, :], ch['gsrc'][:, t0:t0 + 128], reads=[ch['gbuf']], writes=[L_])
                    k.dma('sp', L_[:, 1, :], ch['ksrc'][:, t0:t0 + 128], reads=[ch['kbuf']], writes=[L_])
                    k.dma('sp', L_[:, 2, :], h['q']()[:, t0:t0 + 128], reads=[ch['qbuf']], writes=[L_])
                    vt_ = vld.next()
                    k.dma('sp', vt_[:, 0:dv], gv[t0:t0 + 128, h['vcol']:h['vcol'] + dv], reads=[gv], writes=[vt_])
                    ptr = k.psum()
                    k.op('pe', lambda e: e.transpose(ptr[:, 0:128], L_[:, 0, :], ident_f()), reads=[L_, cp], writes=[ptr], inc=False)
                    k.op('pe', lambda e: e.transpose(ptr[:, 128:256], L_[:, 1, :], ident_f()), reads=[L_, cp], writes=[ptr])
                    tk_ = tk.next()
                    k.op('act', lambda e: e.activation(out=tk_[:].rearrange("p a b -> p (a b)"), in_=ptr[:, 0:256], func=AF.Copy), reads=[ptr], writes=[tk_])
                    CA, CT, CK = (C_AF, C_TRIF, C_KHF) if d == 0 else (C_AB, C_TRIB, C_KHB)
                    pm = k.psum()
                    k.op('pe', lambda e: e.matmul(pm[:, 0:128], lhsT=tk_[:, 0, :], rhs=cp[:, CA:CA + 128], start=True, stop=True), reads=[tk_, cp], writes=[pm], inc=False)
                    k.op('pe', lambda e: e.matmul(pm[:, 128:256], lhsT=tk_[:, 0, :], rhs=cp[:, CT:CT + 128], start=True, stop=True), reads=[tk_, cp], writes=[pm], inc=False)
                    k.op('pe', lambda e: e.matmul(pm[:, 256:384], lhsT=cp[:, CK:CK + 128], rhs=tk_[:, 0, :], start=True, stop=True), reads=[tk_, cp], writes=[pm], inc=False)
                    k.op('pe', lambda e: e.matmul(pm[:, 384:386], lhsT=tk_[:, 0, :], rhs=cp[:, C_TOTC:C_TOTC + 2], start=True, stop=True), reads=[tk_, cp], writes=[pm])
                    ex_ = ex.next()
                    k.op('act', lambda e: e.activation(out=ex_[:, 0, :], in_=pm[:, 0:128], func=AF.Exp), reads=[pm], writes=[ex_])
                    k.op('act', lambda e: e.activation(out=ex_[:, 1, :], in_=pm[:, 0:128], func=AF.Exp, scale=-1.0), reads=[pm, ex_], writes=[ex_])
                    k.op('act', lambda e: e.activation(out=ex_[:, 2:4, :].rearrange("p a b -> p (a b)"), in_=pm[:, 128:384], func=AF.Exp), reads=[pm, ex_], writes=[ex_])
                    dc = dec.next()
                    k.op('act', lambda e: e.activation(out=dc[:], in_=pm[:, 384:386], func=AF.Exp), reads=[pm], writes=[dc])
                    qk_ = qk.next()
                    k.op('dve', lambda e: e.tensor_tensor(out=qk_[:, 0, :], in0=L_[:, 2, :], in1=ex_[:, 0, :], op=ALU.mult), reads=[L_, ex_], writes=[qk_])
                    k.op('dve', lambda e: e.tensor_tensor(out=qk_[:, 1, :], in0=L_[:, 1, :], in1=ex_[:, 1, :], op=ALU.mult), reads=[L_, ex_, qk_], writes=[qk_])
                    k.op('dve', lambda e: e.tensor_tensor(out=qk_[:, 2, :], in0=L_[:, 2, :], in1=ex_[:, 2, :], op=ALU.mult), reads=[L_, ex_, qk_], writes=[qk_])
                    k.op('dve', lambda e: e.tensor_tensor(out=qk_[:, 3, :], in0=tk_[:, 1, :], in1=ex_[:, 3, :], op=ALU.mult), reads=[tk_, ex_, qk_], writes=[qk_])
                    psc = k.psum()
                    k.op('pe', lambda e: e.matmul(psc[:, 0:128], lhsT=qk_[:, 1, :], rhs=qk_[:, 0, :], start=True, stop=True), reads=[qk_], writes=[psc])
                    sm = scm.next()
                    k.op('dve', lambda e: e.tensor_tensor(out=sm[:], in0=psc[:, 0:128], in1=cp[:, CT:CT + 128], op=ALU.mult), reads=[psc, cp], writes=[sm])
                    nvt = dv // 128
                    pos = [k.psum() for _ in range(nvt)]
                    corder = [0, 1] if d == 0 else [1, 0]
                    Sf, Sb = ch['Sf'], ch['Sb']
                    for vi in range(nvt):
                        k.op('pe', lambda e, vi=vi: e.matmul(pos[vi][:, 0:128], lhsT=vt_[:, vi * 128:(vi + 1) * 128], rhs=sm[:], start=True, stop=False), reads=[vt_, sm], writes=[pos[vi]], inc=False)
                    for ci, c_ in enumerate(corder):
                        cs_ = slice(c_ * 64, c_ * 64 + 64)
                        for vi in range(nvt):
                            k.op('pe', lambda e, vi=vi, cs_=cs_: e.matmul(pos[vi][:, cs_], lhsT=Sb[:, vi * 128:(vi + 1) * 128], rhs=qk_[:, 2, cs_], start=False, stop=(ci == 1)), reads=[Sb, qk_], writes=[pos[vi]], inc=(ci == 1 and vi == nvt - 1))
                        pd = k.psum()
                        k.op('pe', lambda e, cs_=cs_: e.matmul(pd[:, 0:dv], lhsT=qk_[cs_, 3, :], rhs=vt_[cs_, 0:dv], start=True, stop=True), reads=[qk_, vt_], writes=[pd])
                        k.op('dve', lambda e, c_=c_: e.scalar_tensor_tensor(out=Sf[:], in0=Sf[:], scalar=dc[:, c_:c_ + 1], in1=pd[:, 0:dv], op0=ALU.mult, op1=ALU.add), reads=[Sf, dc, pd], writes=[Sf])
                        k.op('act', lambda e: e.activation(out=Sb[:], in_=Sf[:], func=AF.Copy), reads=[Sf], writes=[Sb])
                    os_ = osum[ch['h']]
                    key = (ch['h'], tt)
                    for vi in range(nvt):
                        if key not in written:
                            k.op('act', lambda e, vi=vi: e.activation(out=os_[:, vi, t0:t0 + 128], in_=pos[vi][:, 0:128], func=AF.Copy), reads=[pos[vi]], writes=[os_.part(tt)])
                        else:
                            k.op('dve', lambda e, vi=vi: e.tensor_tensor(out=os_[:, vi, t0:t0 + 128], in0=os_[:, vi, t0:t0 + 128], in1=pos[vi][:, 0:128], op=ALU.add), reads=[pos[vi], os_.part(tt)], writes=[os_.part(tt)])
                    written[key] = True
            sqp = ph.pool("sq", [128, TB], BF16, 2)
            rpool = ph.pool("rs", [128, TB], F32, 1)
            gp = ph.pool("gp", [128, TB], F32, 3)
            ost = ph.pool("ost", [128, TB], BF16, 3)
            for hi_, h in enumerate(hd):
                nvt = h['dv'] // 128
                os_ = osum[hi_]
                for sb_ in range(S // TB):
                    t0 = sb_ * TB
                    p2 = k.psum()
                    for vi in range(nvt):
                        sq = sqp.next()
                        k.op('act', lambda e, vi=vi: e.activation(out=sq[:], in_=os_[:, vi, t0:t0 + TB], func=AF.Square), reads=[os_], writes=[sq])
                        k.op('pe', lambda e, vi=vi: e.matmul(p2[:, 0:TB], lhsT=ones_b(), rhs=sq[:], start=(vi == 0), stop=(vi == nvt - 1)), reads=[cpb, sq], writes=[p2], inc=(vi == nvt - 1))
                    r = rstd_from_ps(ph, p2[:, 0:TB], p2, h['dv'], TB, rpool)
                    for vi in range(nvt):
                        g_ = gp.next()
                        gsrc = h['gate'](vi)
                        gb = grT if hi_ == 0 else hoT
                        k.dma('sp', g_[:], gsrc[:, t0:t0 + TB], reads=[gb], writes=[g_])
                        k.op('act', lambda e: e.activation(out=g_[:], in_=g_[:], func=AF.Silu), reads=[g_], writes=[g_])
                        t_ = gp.next()
                        gn, gi = h['gain']
                        k.op('dve', lambda e, vi=vi: e.scalar_tensor_tensor(out=t_[:], in0=os_[:, vi, t0:t0 + TB], scalar=V(gn, gi + vi), in1=r[:, 0:TB], op0=ALU.mult, op1=ALU.mult), reads=[os_, vecs, r], writes=[t_])
                        ob = ost.next()
                        k.op('dve', lambda e: e.tensor_tensor(out=ob[:], in0=t_[:], in1=g_[:], op=ALU.mult), reads=[t_, g_], writes=[ob])
                        o_store(h['orow'] + vi * 128, t0, TB, ob)


    def merge(l):
        for key_ in sorted(o_mine):
            k.allgather(o_mine[key_].t, o_full[key_].t, GROUPS4, reads=[o_mine[key_]], writes=[o_full[key_]])
        with Phase(k) as ph:
            hblk = ph.sb("hblk", [128, KC, TB], BF16)
            ob = ph.sb("ob", [128, 32, TB], BF16)
            mg = ph.sb("mg", [128, KC, TB], BF16)
            stg = ph.pool("stg", [128, 8, TB], BF16, 3)
            wpool = ph.pool("wt", [128, D], BF16, 4)
            bpool = ph.pool("bt", [128, 2048], BF16, 2)
            bpool2 = ph.pool("bt2", [128, 1024], BF16, 3)
            sgp = ph.pool("sg", [128, TB], F32, 4)
            mp = ph.pool("mp", [128, TB], F32, 3)
            xpool = ph.pool("xc", [128, TB], F32, 3)
            for tb in range(cfg.NTB):
                t0 = tb * TB
                for j in range(NHC):
                    k.dma('sp', hblk[:, j * (HR // 128):(j + 1) * (HR // 128), :], hT_own[j][:, t0:t0 + TB].rearrange("(c p) t -> p c t", p=128), reads=[hT_own[j]], writes=[hblk])
                for s in range(4):
                    for r2 in range(4):
                        st = stg.next()
                        for q in range(NOP):
                            k.dma('sp', st[:, q * (OR // 128):(q + 1) * (OR // 128), :], o_full[s, q][r2 * OR:(r2 + 1) * OR, t0:t0 + TB].rearrange("(c p) t -> p c t", p=128), reads=[o_full[s, q]], writes=[st])
                        dst = ob[:, r2 * 8:(r2 + 1) * 8, :]
                        if s == 0:
                            k.op('dve', lambda e: e.tensor_scalar(out=dst, in0=st[:], scalar1=V('sel', s), scalar2=None, op0=ALU.mult), reads=[st, vecs], writes=[ob.part(r2)])
                        else:
                            k.op('dve', lambda e: e.scalar_tensor_tensor(out=dst, in0=st[:], scalar=V('sel', s), in1=dst, op0=ALU.mult, op1=ALU.add), reads=[st, vecs, ob.part(r2)], writes=[ob.part(r2)])
                for i in range(KC):
                    ws = []
                    for gi in range(3):
                        w = wpool.next()
                        wtile('gat', l, gi * KC + i, w)
                        ws.append(w)
                    ba = bpool.next()
                    wtile('bra', l, i, ba)
                    bg, bh = bpool2.next(), bpool2.next()
                    wtile('brg', l, i, bg)
                    wtile('brh', l, i, bh)
                    pgs = []
                    for gi in range(3):
                        p = k.psum()
                        for c in range(KC):
                            k.op('pe', lambda e, c=c, gi=gi: e.matmul(p[:, 0:TB], lhsT=ws[gi][:, c * 128:(c + 1) * 128], rhs=hblk[:, c, :], start=(c == 0), stop=(c == KC - 1)), reads=[ws[gi], hblk], writes=[p], inc=(c == KC - 1))
                        pgs.append(p)
                    pys = []
                    for bi, (bw, nk, fidx) in enumerate(((ba, 16, lambda kc: (kc // 4) * 8 + kc % 4), (bg, 8, lambda kc: (kc // 2) * 8 + 4 + kc % 2), (bh, 8, lambda kc: (kc // 2) * 8 + 6 + kc % 2))):
                        p = k.psum()
                        for kc in range(nk):
                            k.op('pe', lambda e, kc=kc: e.matmul(p[:, 0:TB], lhsT=bw[:, kc * 128:(kc + 1) * 128], rhs=ob[:, fidx(kc), :], start=(kc == 0), stop=(kc == nk - 1)), reads=[bw, ob], writes=[p], inc=(kc == nk - 1))
                        pys.append(p)
                    ms = []
                    for gi in range(3):
                        sg = sgp.next()
                        k.op('act', lambda e, gi=gi: e.activation(out=sg[:], in_=pgs[gi][:, 0:TB], func=AF.Sigmoid), reads=[pgs[gi]], writes=[sg])
                        m_ = mp.next()
                        k.op('dve', lambda e, gi=gi: e.tensor_tensor(out=m_[:], in0=sg[:], in1=pys[gi][:, 0:TB], op=ALU.mult), reads=[sg, pys[gi]], writes=[m_])
                        ms.append(m_)
                    k.op('dve', lambda e: e.tensor_tensor(out=ms[0][:], in0=ms[0][:], in1=ms[1][:], op=ALU.add), reads=[ms[0], ms[1]], writes=[ms[0]])
                    k.op('dve', lambda e, i=i: e.tensor_tensor(out=mg[:, i, :], in0=ms[0][:], in1=ms[2][:], op=ALU.add), reads=[ms[0], ms[2]], writes=[mg.part(i)])
                for i in range(KC):
                    w = wpool.next()
                    wtile('wo', l, i, w)
                    po = k.psum()
                    for c in range(KC):
                        k.op('pe', lambda e, c=c: e.matmul(po[:, 0:TB], lhsT=w[:, c * 128:(c + 1) * 128], rhs=mg[:, c, :], start=(c == 0), stop=(c == KC - 1)), reads=[w, mg.part(c)], writes=[po], inc=(c == KC - 1))
                    xc = xpool.next()
                    k.dma('sp', xc[:], xT[i * 128:(i + 1) * 128, t0:t0 + TB], reads=[xT.part(i)], writes=[xc])
                    k.op('dve', lambda e: e.tensor_tensor(out=xc[:], in0=po[:, 0:TB], in1=xc[:], op=ALU.add), reads=[po, xc], writes=[xc])
                    k.dma('pool', xT[i * 128:(i + 1) * 128, t0:t0 + TB], xc[:], reads=[xc], writes=[xT.part(i)])

    seq = []
    for l in range(L):
        seq += [('ffn1', lambda l=l: (ffn(l, 1, xT, xT), weight_prep(l + 1) if l + 1 < L else None)), ('mixnorm', lambda l=l: mixnorm(l)), ('proj', lambda l=l: projection(l)),
                ('attn', lambda l=l: attention(l)), ('scan', lambda l=l: scans(l)), ('merge', lambda l=l: merge(l)),
                ('ffn2', lambda l=l: ffn(l, 2, xT, yT if (l == L - 1 and STOP is None) else xT))]
    done = False
    if STOP == 'prep':
        done = True
    for nm, fn in seq:
        if done:
            break
        if STOP is not None and STOP.startswith('-') and nm == STOP[1:]:
            break
        fn()
        if nm == STOP:
            done = True
    if STOP is not None:
        k.dma('pool', yT[:, :], xT[:, :], reads=[xT], writes=[yT])
    k.barrier()
    pp.__exit__(None, None, None)
    return nc


PROJ_OFF = dict(a_q=0, a_k=2048, a_v=4096, g_q=6144, g_k=6656, g_v=7168, g_r=8192, lr_f=9216, lr_b=9232,
                h_q=9248, h_zf=10272, h_zb=11296, h_i=12320, h_g=13344, gate=14368)


def kernel(**inp):
    inp = {kk: np.asarray(v) for kk, v in inp.items()}
    x = inp['x']
    B, S, D = x.shape
    L = inp['ffn1_norm'].shape[0]
    F = inp['ffn1_w_out'].shape[1]
    cfg = Cfg(D=D, S=S, L=L, F=F)
    T, KC = cfg.T, cfg.KC
    voff, NV = vec_layout(cfg)
    nc = build(cfg)
    f32 = np.float32
    w_in = inp['w_in']

    packed = {}
    for l in range(L):
        packed['f1i', l] = pack_fm(inp['ffn1_w_in'][l])
        packed['f1o', l] = pack_fm(inp['ffn1_w_out'][l])
        packed['gat', l] = pack_fm(w_in[l][:, PROJ_OFF['gate']:PROJ_OFF['gate'] + 3 * D])
        packed['bra', l] = pack_fm(inp['w_branch_attn'][l])
        packed['brg', l] = pack_fm(inp['w_branch_gla'][l])
        packed['brh', l] = pack_fm(inp['w_branch_hgrn'][l])
        packed['wo', l] = pack_fm(inp['w_out'][l])
        packed['f2i', l] = pack_fm(inp['ffn2_w_in'][l])
        packed['f2o', l] = pack_fm(inp['ffn2_w_out'][l])
    kinds = ['f1i', 'f1o', 'gat', 'bra', 'brg', 'brh', 'wo', 'f2i', 'f2o']
    cpack = const_pack(cfg)
    cossin = cossin_tables(cfg)
    in_maps = []
    for c in range(8):
        b, r = c // 4, c % 4
        m = {}
        m['xT'] = np.ascontiguousarray(x[b, r * T:(r + 1) * T, :].T.astype(f32))
        for kn in kinds:
            parts = []
            for l in range(L):
                P_ = packed[kn, l]
                NT_ = P_.shape[0] // 128
                parts.append(P_.reshape(NT_, 4, 32, P_.shape[1])[:, r].reshape(NT_ * 32, P_.shape[1]))
            m['wsh_' + kn] = np.ascontiguousarray(np.concatenate(parts, 0))
        fm_tiles = []
        tm_groups = []
        for l in range(L):
            W = w_in[l]
            cols = []
            for hh in range(2):
                for mm in range(2):
                    cols.append((PROJ_OFF['a_q'] + (2 * r + hh) * 256 + mm * 128, 128))
            for hh in range(2):
                for mm in range(2):
                    cols.append((PROJ_OFF['a_k'] + (2 * r + hh) * 256 + mm * 128, 128))
            cols.append((PROJ_OFF['g_q'] + r * 128, 128))
            cols.append((PROJ_OFF['g_k'] + r * 128, 128))
            cols.append((PROJ_OFF['g_r'] + r * 256, 128))
            cols.append((PROJ_OFF['g_r'] + r * 256 + 128, 128))
            cols.append((PROJ_OFF['lr_f'], 16))
            cols.append((PROJ_OFF['lr_b'], 16))
            for nm in ('h_q', 'h_zf', 'h_zb', 'h_g'):
                for hh in range(2):
                    cols.append((PROJ_OFF[nm] + (2 * r + hh) * 128, 128))
            assert len(cols) == cfg.NFM
            for (c0, w) in cols:
                fm_tiles.append(pack_fm(pad_cols(W[:, c0:c0 + w], 128)))
            g0 = W[:, PROJ_OFF['a_v'] + 2 * r * 256: PROJ_OFF['a_v'] + (2 * r + 2) * 256]
            g1 = np.concatenate([W[:, PROJ_OFF['g_v'] + r * 256: PROJ_OFF['g_v'] + (r + 1) * 256],
                                 W[:, PROJ_OFF['h_i'] + 2 * r * 128: PROJ_OFF['h_i'] + (2 * r + 2) * 128]], 1)
            tm_groups.append(pack_tm(g0))
            tm_groups.append(pack_tm(g1))
        m['hfm'] = np.ascontiguousarray(np.concatenate(fm_tiles, 0))
        m['htm'] = np.ascontiguousarray(np.concatenate(tm_groups, 0))
        v = np.zeros((128, NV), f32)

        def put(nm, idx, col):
            v[:, voff[nm] + idx] = col
        for l in range(L):
            for cc in range(KC):
                put('n1', l * KC + cc, inp['ffn1_norm'][l, cc * 128:(cc + 1) * 128])
                put('nm', l * KC + cc, inp['mix_norm'][l, cc * 128:(cc + 1) * 128])
                put('n2', l * KC + cc, inp['ffn2_norm'][l, cc * 128:(cc + 1) * 128])
            put('qn', l, inp['attn_q_norm'][l])
            put('kn', l, inp['attn_k_norm'][l])
            for j in range(4):
                put('lam', l * 4 + j, inp['attn_lambda'][l, j])
            for et in range(2):
                put('sub', l * 2 + et, inp['attn_sub_norm'][l, et * 128:(et + 1) * 128])
                put('gon', l * 2 + et, inp['gla_out_norm'][l, et * 128:(et + 1) * 128])
            put('hon', l, inp['hgrn_out_norm'][l])
            for d, nm in enumerate(('fwd', 'bwd')):
                put('gbias', l * 2 + d, inp['gla_gate_b_' + nm][l, r * 128:(r + 1) * 128])
                v[0:16, voff['w2'] + (l * 2 + d) * 128: voff['w2'] + (l * 2 + d + 1) * 128] = inp['gla_gate_w2_' + nm][l][:, r * 128:(r + 1) * 128]
                for hh in range(2):
                    put('lb', (d * 2 + hh) * L + l, inp['hgrn_lb_' + nm][l, (2 * r + hh) * 128:(2 * r + hh + 1) * 128])
        v[:, voff['sel'] + r] = 1.0
        m['vecs'] = v
        m['cpack'] = cpack
        m['cossin'] = cossin
        in_maps.append(m)
    res = run_bass_kernel_spmd(nc, in_maps, core_ids=list(range(8)))
    out = np.zeros((B, S, D), f32)
    for c in range(8):
        b, r = c // 4, c % 4
        out[b, r * T:(r + 1) * T, :] = res.results[c]['yT'].T
    return out
```
